# Optimizing a Trainium2 kernel written in Bass

```python
import math
import jax, jax.numpy as jnp
from jax import lax
import numpy as np

D_MODEL = 1024
BATCH = 16
SEQ = 4096
DEPTH = 4

GRID_W = 64
NA_HEADS = 8
NA_HEAD_DIM = 64
NA_WIDTH = NA_HEADS * NA_HEAD_DIM
NA_WIN_R = 8
NA_WIN_C = 16
DA_HEADS = 4
DA_HEAD_DIM = 64
DA_V_DIM = 2 * DA_HEAD_DIM
DA_WIDTH = DA_HEADS * DA_V_DIM
MIX_WIDTH = NA_WIDTH + DA_WIDTH
IN_COLS = 4 * NA_WIDTH + 4 * DA_WIDTH
Q_BLOCK = 128
T5_BUCKETS = 32
T5_MAX_EXACT = 8
T5_MAX_DIST = 128
NORM_EPS = 1e-6
SUBLN_EPS = 1e-5

kernel_name = "hybrid_natten_diffattn_encoder"


def rmsnorm(x, g, eps=NORM_EPS):
    xf = x.astype(jnp.float32)
    y = xf * lax.rsqrt(jnp.mean(xf * xf, axis=-1, keepdims=True) + eps)
    return (y * g.astype(jnp.float32)).astype(x.dtype)


def t5_bucket(rel):
    n = T5_BUCKETS // 2
    ret = jnp.where(rel > 0, n, 0)
    a = jnp.abs(rel)
    small = a < T5_MAX_EXACT
    af = jnp.maximum(a, 1).astype(jnp.float32)
    large = T5_MAX_EXACT + (jnp.log(af / T5_MAX_EXACT) / math.log(T5_MAX_DIST / T5_MAX_EXACT)
                            * (n - T5_MAX_EXACT)).astype(jnp.int32)
    large = jnp.minimum(large, n - 1)
    return ret + jnp.where(small, a, large)


def neighborhood_attention(q, k, v, rpb):
    b, s, h, dh = q.shape
    rows = s // GRID_W
    wr = min(NA_WIN_R, rows)
    scale = dh ** -0.5

    def to_grid(t):
        return t.reshape(b, rows, GRID_W, h, dh).transpose(0, 3, 1, 2, 4)

    qg, kg, vg = to_grid(q), to_grid(k), to_grid(v)
    c = jnp.arange(GRID_W)
    cs = jnp.clip(c - NA_WIN_C // 2, 0, GRID_W - NA_WIN_C)
    col_idx = cs[:, None] + jnp.arange(NA_WIN_C)[None, :]
    col_bias_idx = col_idx - c[:, None] + (NA_WIN_C - 1)

    def row_step(r):
        rs = jnp.clip(r - wr // 2, 0, rows - wr)
        qr = lax.dynamic_index_in_dim(qg, r, axis=2, keepdims=False)
        kr = lax.dynamic_slice_in_dim(kg, rs, wr, axis=2)[:, :, :, col_idx]
        vr = lax.dynamic_slice_in_dim(vg, rs, wr, axis=2)[:, :, :, col_idx]
        row_bias_idx = rs + jnp.arange(wr) - r + (NA_WIN_R - 1)
        bias = rpb[:, row_bias_idx][:, :, col_bias_idx]
        bias = bias.transpose(0, 2, 1, 3).astype(jnp.float32)
        logits = jnp.einsum('bhcd,bhicjd->bhcij', qr, kr).astype(jnp.float32) * scale + bias[None]
        p = jax.nn.softmax(logits.reshape(b, h, GRID_W, wr * NA_WIN_C), axis=-1)
        p = p.reshape(b, h, GRID_W, wr, NA_WIN_C).astype(vr.dtype)
        return jnp.einsum('bhcij,bhicjd->bhcd', p, vr)

    out = lax.map(row_step, jnp.arange(rows))
    return out.transpose(1, 0, 3, 2, 4).reshape(b, s, h * dh)


def diff_attention(q, k, v, t5_table, lam, lam_init, subln_g):
    b, s, h, _, dh = q.shape
    nb = s // Q_BLOCK
    scale = dh ** -0.5
    qb = q.reshape(b, nb, Q_BLOCK, h, 2, dh).transpose(1, 0, 2, 3, 4, 5)
    kpos = jnp.arange(s)

    def block_step(args):
        qblk, i = args
        qpos = i * Q_BLOCK + jnp.arange(Q_BLOCK)
        bias = t5_table[t5_bucket(kpos[None, :] - qpos[:, None])]
        bias = bias.transpose(2, 0, 1).astype(jnp.float32)
        logits = jnp.einsum('bqhtd,bkhtd->bhtqk', qblk, k).astype(jnp.float32) * scale
        p = jax.nn.softmax(logits + bias[None, :, None], axis=-1)
        attn = (p[:, :, 0] - lam * p[:, :, 1]).astype(v.dtype)
        return jnp.einsum('bhqk,bkhe->bqhe', attn, v)

    out = lax.map(block_step, (qb, jnp.arange(nb)))
    out = out.transpose(1, 0, 2, 3, 4).reshape(b, s, h, DA_V_DIM)
    out = rmsnorm(out, subln_g, eps=SUBLN_EPS) * (1.0 - lam_init)
    return out.reshape(b, s, h * DA_V_DIM)


def setup_inputs(seed: int = 0) -> dict:
    key = jax.random.key(seed)
    ks = jax.random.split(key, 12)
    f32 = jnp.float32
    x = jax.random.normal(ks[0], (BATCH, SEQ, D_MODEL), f32)
    norm_g = 1.0 + 0.01 * jax.random.normal(ks[1], (DEPTH, D_MODEL), f32)
    w_in = jax.random.normal(ks[2], (DEPTH, D_MODEL, IN_COLS), f32) * D_MODEL ** -0.5
    na_rpb = 0.1 * jax.random.normal(ks[3], (DEPTH, NA_HEADS, 2 * NA_WIN_R - 1, 2 * NA_WIN_C - 1), f32)
    lambda_q1 = 0.1 * jax.random.normal(ks[4], (DEPTH, DA_HEAD_DIM), f32)
    lambda_k1 = 0.1 * jax.random.normal(ks[5], (DEPTH, DA_HEAD_DIM), f32)
    lambda_q2 = 0.1 * jax.random.normal(ks[6], (DEPTH, DA_HEAD_DIM), f32)
    lambda_k2 = 0.1 * jax.random.normal(ks[7], (DEPTH, DA_HEAD_DIM), f32)
    subln_g = 1.0 + 0.01 * jax.random.normal(ks[8], (DEPTH, DA_V_DIM), f32)
    t5_table = 0.1 * jax.random.normal(ks[9], (T5_BUCKETS, DA_HEADS), f32)
    w_out = jax.random.normal(ks[10], (DEPTH, MIX_WIDTH, D_MODEL), f32) * MIX_WIDTH ** -0.5
    final_g = 1.0 + 0.01 * jax.random.normal(ks[11], (D_MODEL,), f32)
    return {"x": x, "norm_g": norm_g, "w_in": w_in, "na_rpb": na_rpb,
            "lambda_q1": lambda_q1, "lambda_k1": lambda_k1,
            "lambda_q2": lambda_q2, "lambda_k2": lambda_k2,
            "subln_g": subln_g, "t5_table": t5_table,
            "w_out": w_out, "final_g": final_g}


def reference(x, norm_g, w_in, na_rpb, lambda_q1, lambda_k1, lambda_q2, lambda_k2,
              subln_g, t5_table, w_out, final_g):
    b, s, _ = x.shape
    A = NA_WIDTH
    o_b = 4 * NA_WIDTH
    Bw = DA_WIDTH
    for l in range(DEPTH):
        h = rmsnorm(x, norm_g[l])
        proj = h @ w_in[l]
        q_a = proj[..., 0:A].reshape(b, s, NA_HEADS, NA_HEAD_DIM)
        k_a = proj[..., A:2 * A].reshape(b, s, NA_HEADS, NA_HEAD_DIM)
        v_a = proj[..., 2 * A:3 * A].reshape(b, s, NA_HEADS, NA_HEAD_DIM)
        g_a = proj[..., 3 * A:4 * A]
        out_a = neighborhood_attention(q_a, k_a, v_a, na_rpb[l])
        q_b = proj[..., o_b:o_b + Bw].reshape(b, s, DA_HEADS, 2, DA_HEAD_DIM)
        k_b = proj[..., o_b + Bw:o_b + 2 * Bw].reshape(b, s, DA_HEADS, 2, DA_HEAD_DIM)
        v_b = proj[..., o_b + 2 * Bw:o_b + 3 * Bw].reshape(b, s, DA_HEADS, DA_V_DIM)
        g_b = proj[..., o_b + 3 * Bw:o_b + 4 * Bw]
        lam_init = 0.8 - 0.6 * math.exp(-0.3 * l)
        lam = (jnp.exp(jnp.sum(lambda_q1[l].astype(jnp.float32) * lambda_k1[l].astype(jnp.float32)))
               - jnp.exp(jnp.sum(lambda_q2[l].astype(jnp.float32) * lambda_k2[l].astype(jnp.float32)))
               + lam_init)
        out_b = diff_attention(q_b, k_b, v_b, t5_table, lam, lam_init, subln_g[l])
        y = jnp.concatenate([out_a * jax.nn.silu(g_a), out_b * jax.nn.silu(g_b)], axis=-1)
        x = x + y @ w_out[l]
    return rmsnorm(x, final_g)
```

```python
import math
from contextlib import ExitStack

import numpy as np
import concourse.bass as bass
import concourse.mybir as mybir
from concourse.bass_utils import run_bass_kernel_spmd

F32 = mybir.dt.float32
BF16 = mybir.dt.bfloat16
AF = mybir.ActivationFunctionType
ALU = mybir.AluOpType
AX = mybir.AxisListType

S = 4096
D = 1024
NT = S // 128
GW = 64
DEPTH = 4
NCORES = 8
NEG = -30000.0


def _t5_bucket(rel):
    n = 16
    ret = np.where(rel > 0, n, 0)
    a = np.abs(rel)
    small = a < 8
    af = np.maximum(a, 1).astype(np.float32)
    v = (np.log(af / np.float32(8)) / np.float32(math.log(16.0)) * np.float32(8)).astype(np.float32)
    large = np.minimum(8 + v.astype(np.int32), n - 1)
    return ret + np.where(small, a, large)


def _t5_strip_idx():
    kl = np.arange(128)[:, None]
    uu = np.arange(896)[None, :]
    return _t5_bucket(kl - (uu - 384))


def _na_kts(t):
    r0, r1 = 2 * t, 2 * t + 1
    rs0 = min(max(r0 - 4, 0), 56)
    rs1 = min(max(r1 - 4, 0), 56)
    lo, hi = min(rs0, rs1), max(rs0, rs1) + 7
    return list(range(lo // 2, hi // 2 + 1))


def _na_tables():
    kl = np.arange(128)
    krl, kc = kl // 64, kl % 64
    ql = np.arange(128)
    qrl, qc = ql // 64, ql % 64
    cs = np.clip(qc - 8, 0, GW - 16)
    colok = (kc[:, None] >= cs[None, :]) & (kc[:, None] < cs[None, :] + 16)
    dc = kc[:, None] - qc[None, :]
    ridx = np.full((7, 128, 8, 128), -1, np.int64)
    for oi in range(7):
        o = oi - 3
        dr = 2 * o + krl[:, None] - qrl[None, :]
        ok = (dr + 7 >= 0) & (dr + 7 <= 14) & (dc + 15 >= 0) & (dc + 15 <= 30)
        base = (dr + 7) * 31 + (dc + 15)
        for h in range(8):
            ridx[oi, :, h, :] = np.where(ok, h * 15 * 31 + base, -1)
    ridx = ridx.reshape(7, 128, 4, 2, 128).transpose(0, 1, 3, 2, 4).reshape(7, 128, 1024)
    masks = []
    combo_of = {}
    key2id = {}
    for t in range(NT):
        for kt in _na_kts(t):
            qr = 2 * t + qrl
            rs = np.clip(qr - 4, 0, 56)
            kr = 2 * kt + krl
            rowok = (kr[:, None] >= rs[None, :]) & (kr[:, None] < rs[None, :] + 8)
            m = np.where(rowok & colok, 0.0, NEG).astype(np.float32)
            key = (kt - t, m.tobytes())
            if key not in key2id:
                key2id[key] = len(masks)
                masks.append((kt - t + 3, m))
            combo_of[(t, kt)] = key2id[key]
    return ridx, masks, combo_of


_RIDX, _MASKS, _COMBO = _na_tables()
NCOMBO = len(_MASKS)


class Sem:
    def __init__(self, nc, stack, name):
        self.h = stack.enter_context(nc.semaphore(name))
        self.n = 0
        self.name = name


class Ctx:
    def __init__(self, nc, stack):
        self.nc = nc
        self.stack = stack
        self.waited = {}
        self.junk_tk = None
        self.prev_chain = None
        self.prev_chain_a = None
        self.nsem = 0
        self.eng = {"pe": nc.tensor, "act": nc.scalar, "dve": nc.vector, "pool": nc.gpsimd, "sp": nc.sync}

    def sem(self, name):
        self.nsem += 1
        return Sem(self.nc, self.stack, name)

    def inc(self, ins, sem, amt=1):
        ins.then_inc(sem.h, amt)
        sem.n += amt
        return (sem, sem.n)

    def dinc(self, ins, sem):
        return self.inc(ins, sem, 16)

    def wait(self, en, tk):
        if tk is None:
            return
        if isinstance(tk, list):
            for t in tk:
                self.wait(en, t)
            return
        sem, val = tk
        key = (en, sem.name)
        if self.waited.get(key, 0) >= val:
            return
        self.waited[key] = val
        self.eng[en].wait_ge(sem.h, val)


def build_nc(nseq=2, nlayers=DEPTH, final=True, debug=False):
    nc = bass.Bass("TRN2", target_bir_lowering=False)
    NTOK = nseq * S
    x_in = nc.dram_tensor("x", [NTOK, D], F32, kind="ExternalInput").ap()
    w_in = nc.dram_tensor("w_in", [nlayers, D, 4096], F32, kind="ExternalInput").ap()
    w_out = nc.dram_tensor("w_out", [nlayers, D, D], F32, kind="ExternalInput").ap()
    norm_g = nc.dram_tensor("norm_g", [nlayers, D], F32, kind="ExternalInput").ap()
    final_g = nc.dram_tensor("final_g", [1, D], F32, kind="ExternalInput").ap()
    subln_g = nc.dram_tensor("subln_g", [nlayers, 128], F32, kind="ExternalInput").ap()
    lamv = nc.dram_tensor("lamv", [nlayers, 4 * 64], F32, kind="ExternalInput").ap()
    t5_strip = nc.dram_tensor("t5_strip", [128, 4, 896], F32, kind="ExternalInput").ap()
    t5_far = nc.dram_tensor("t5_far", [1, 8], F32, kind="ExternalInput").ap()
    na_R = nc.dram_tensor("na_R", [nlayers, 128, 7, 1024], F32, kind="ExternalInput").ap()
    na_mask = nc.dram_tensor("na_mask", [128, NCOMBO, 128], F32, kind="ExternalInput").ap()
    ident_in = nc.dram_tensor("ident", [128, 128], F32, kind="ExternalInput").ap()
    out = nc.dram_tensor("out", [NTOK, D], F32, kind="ExternalOutput").ap()
    okind = "ExternalOutput" if debug else "Internal"
    xs = nc.dram_tensor("xs", [NTOK, D], F32, kind="Internal").ap()
    QTa = [nc.dram_tensor(f"QTa{s}", [128, 4, S], BF16, kind=okind).ap() for s in range(nseq)]
    KTa = [nc.dram_tensor(f"KTa{s}", [128, 4, S], BF16, kind=okind).ap() for s in range(nseq)]
    QTb = [nc.dram_tensor(f"QTb{s}", [128, 4, S], BF16, kind=okind).ap() for s in range(nseq)]
    KTb = [nc.dram_tensor(f"KTb{s}", [128, 4, S], BF16, kind=okind).ap() for s in range(nseq)]
    Va = [nc.dram_tensor(f"Va{s}", [S, 520], BF16, kind=okind).ap() for s in range(nseq)]
    Vb = [nc.dram_tensor(f"Vb{s}", [S, 516], BF16, kind=okind).ap() for s in range(nseq)]
    Gs = [nc.dram_tensor(f"G{s}", [S, D], F32, kind=okind).ap() for s in range(nseq)]
    Ys = [nc.dram_tensor(f"Y{s}", [S, D], BF16, kind=okind).ap() for s in range(nseq)]

    with ExitStack() as stack:
        cx = Ctx(nc, stack)
        E = cx.eng
        pe, act, dve, pool, sp = E["pe"], E["act"], E["dve"], E["pool"], E["sp"]

        uniq = [0]

        def sb(name, shape, dt, st=stack):
            uniq[0] += 1
            return st.enter_context(nc.sbuf_tensor(f"sb{uniq[0]}_{name}", shape, dt))

        ps = stack.enter_context(nc.psum_tensor("ps", [128, 4096], F32))
        stack.enter_context(nc.Block())

        def bank(b, n=1):
            return ps[:, b * 512:(b + n) * 512]

        ident = sb("ident", [128, 128], BF16)
        strip = sb("strip", [128, 4, 896], BF16)
        far = sb("far", [128, 8], F32)
        zero1 = sb("zero1", [128, 1], F32)
        maskb = sb("maskb", [128, NCOMBO, 128], BF16)
        fg_rep = sb("fg_rep", [128, D], F32)
        g_rep = sb("g_rep", [128, D], F32)
        sgb = sb("sgb", [128, 4, 128], F32)
        lamt = sb("lamt", [128, 4 * 64], F32)
        lamw = sb("lamw", [128, 8], F32)
        junk = sb("junk", [128, D], BF16)
        mhalf = sb("mhalf", [128, NT], F32)

        s_setup = cx.sem("setup")
        s_bar = cx.sem("bar")

        def barrier():
            for en in ("pe", "act", "dve", "pool", "sp"):
                cx.inc(E[en].drain(), s_bar)
            tk = (s_bar, s_bar.n)
            for en in ("pe", "act", "dve", "pool", "sp"):
                cx.wait(en, tk)

        s_setup_sw = cx.sem("setupsw")
        cx.dinc(pool.dma_start(out=ident[:], in_=ident_in), s_setup_sw)
        cx.dinc(pool.dma_start(out=strip[:], in_=t5_strip), s_setup_sw)
        cx.dinc(pool.dma_start(out=maskb[:], in_=na_mask), s_setup_sw)
        cx.dinc(sp.dma_start(out=far[:], in_=t5_far.partition_broadcast(128)), s_setup)
        cx.dinc(sp.dma_start(out=fg_rep[:], in_=final_g.partition_broadcast(128)), s_setup)
        setup_tk = [(s_setup, s_setup.n), (s_setup_sw, s_setup_sw.n)]
        s_ms = cx.sem("ms")
        dve.memset(mhalf[:], -0.5)
        ms_tk = cx.inc(dve.memset(zero1[:], 0.0), s_ms)
        for en in ("pe", "act", "dve", "pool", "sp"):
            cx.wait(en, setup_tk)
            cx.wait(en, ms_tk)

        s_lay = cx.sem("lay")
        s_layc = cx.sem("layc")
        s_w1 = cx.sem("w1")
        cx.s_w1g = [cx.sem(f"w1g{i}") for i in range(8)]
        s_wo = cx.sem("wo")

        xsrc = x_in
        for l in range(nlayers):
            lam_init = 0.8 - 0.6 * math.exp(-0.3 * l)
            last_layer = (l == nlayers - 1)
            cx.dinc(sp.dma_start(out=g_rep[:], in_=norm_g[l:l + 1, :].partition_broadcast(128)), s_lay)
            cx.dinc(sp.dma_start(out=lamt[:], in_=lamv[l:l + 1, :].partition_broadcast(128)), s_lay)
            tk = cx.dinc(sp.dma_start(out=sgb[:, 0, :], in_=subln_g[l:l + 1, :].partition_broadcast(128)), s_lay)
            cx.wait("dve", tk)
            cx.wait("dve", cx.junk_tk)
            t1 = cx.inc(dve.tensor_tensor(out=junk[:, 0:64], in0=lamt[:, 0:64], in1=lamt[:, 64:128], op=ALU.mult), s_layc)
            t2 = cx.inc(dve.tensor_tensor(out=junk[:, 64:128], in0=lamt[:, 128:192], in1=lamt[:, 192:256], op=ALU.mult), s_layc)
            cx.wait("dve", t2)
            t2b = cx.inc(dve.reduce_sum(out=lamw[:, 0:1], in_=junk[:, 0:64], axis=AX.X), s_layc)
            cx.wait("dve", t2b)
            t3 = cx.inc(dve.reduce_sum(out=lamw[:, 1:2], in_=junk[:, 64:128], axis=AX.X), s_layc)
            cx.junk_tk = t3
            cx.wait("act", t3)
            t4 = cx.inc(act.activation(out=lamw[:, 2:4], in_=lamw[:, 0:2], func=AF.Exp), s_layc)
            cx.wait("dve", t4)
            t5 = cx.inc(dve.tensor_tensor(out=lamw[:, 5:6], in0=lamw[:, 2:3], in1=lamw[:, 3:4], op=ALU.subtract), s_layc)
            cx.wait("dve", t5)
            t6 = cx.inc(dve.tensor_scalar(out=lamw[:, 4:5], in0=lamw[:, 5:6], scalar1=float(lam_init), scalar2=None, op0=ALU.add), s_layc)
            t7 = cx.inc(dve.tensor_scalar(out=sgb[:, 0, :], in0=sgb[:, 0, :], scalar1=float(1.0 - lam_init), scalar2=None, op0=ALU.mult), s_layc)
            cx.wait("dve", t7)
            for hh in range(1, 4):
                t8 = cx.inc(dve.tensor_copy(out=sgb[:, hh, :], in_=sgb[:, 0, :]), s_layc)
            lay_tk = [t6, t8, (s_lay, s_lay.n)]
            for en in ("pe", "act", "dve", "pool", "sp"):
                cx.wait(en, lay_tk)

            with ExitStack() as p1:
                W1 = sb("W1", [128, 8, 4096], BF16, p1)
                xt = [sb(f"xt{i}", [128, D], F32, p1) for i in range(4)]
                hb = [sb(f"hb{i}", [128, D], BF16, p1) for i in range(4)]
                hT = [sb(f"hT{i}", [128, 8, 512], BF16, p1) for i in range(2)]
                stgF = [sb(f"stgF{i}", [128, 4, 512], BF16, p1) for i in range(3)]
                stgVa = [sb(f"stgVa{i}", [128, 8, 65], BF16, p1) for i in range(2)]
                stgVb = [sb(f"stgVb{i}", [128, 4, 129], BF16, p1) for i in range(2)]
                stgG = [sb(f"stgG{i}", [128, D], F32, p1) for i in range(2)]
                ssq = sb("ssq", [128, NT], F32, p1)
                rtmp = sb("rtmp", [128, NT], F32, p1)
                rstd = sb("rstd", [128, NT], F32, p1)
                if l == 0:
                    s_p1 = {k: cx.sem("p1" + k) for k in
                            ("ss", "rs", "h", "tr", "hte", "fmm", "tmm", "fevdve", "fevact", "tevdve", "tevact", "ones")}
                    s_p1["ldx"] = [cx.sem(f"p1ldx{i}") for i in range(4)]
                    s_p1["stF"] = [cx.sem(f"p1stF{i}") for i in range(3)]
                    s_p1["stV"] = [cx.sem(f"p1stV{i}") for i in range(2)]
                    cx.s_p1 = s_p1
                s_p1 = cx.s_p1
                for i in range(2):
                    pool.memset(stgVa[i][:, :, 64:65], 1.0)
                    ones_tk = cx.inc(pool.memset(stgVb[i][:, :, 128:129], 1.0), s_p1["ones"])
                cx.wait("dve", ones_tk)
                cx.wait("act", ones_tk)
                cx.wait("sp", ones_tk)

                rel_xt = {}
                rel_hb = {}
                rel_tr = {}
                rel_hT = {}
                rel_psF = {}
                rel_psT = {}
                rel_stF = {}
                rel_stV = {}
                cnt = {"nF": 0, "nT": 0, "nSF": 0, "nSV": 0}
                hte_last = {}
                h_tk = {}
                a1_tk = {}
                NX = len(xt)
                NH = len(hb)

                ldx_of = {}

                def stageL(s, b):
                    ldxs = {}
                    for ii in range(4):
                        i = b * 4 + ii
                        u = s * NT + i
                        tok0 = s * S + i * 128
                        cx.wait("pool", rel_xt.get(u - NX))
                        ldxs[ii] = cx.dinc(pool.dma_start(out=xt[u % NX][:], in_=xsrc[tok0:tok0 + 128, :]), s_p1["ldx"][u % NX])
                    ldx_of[(s, b)] = ldxs

                def stageA(s, b):
                    if (s, b) not in ldx_of:
                        stageL(s, b)
                    ldxs = ldx_of[(s, b)]
                    ssts = {}
                    for ii in range(4):
                        i = b * 4 + ii
                        u = s * NT + i
                        cx.wait("act", ldxs[ii])
                        cx.wait("act", cx.junk_tk)
                        ssts[ii] = cx.inc(act.activation(out=junk[:], in_=xt[u % NX][:], func=AF.Square,
                                                         accum_out=ssq[:, i:i + 1]), s_p1["ss"])
                        cx.junk_tk = ssts[ii]
                    i0 = b * 4
                    cx.wait("pool", ssts[3])
                    r1 = cx.inc(pool.tensor_scalar(out=rtmp[:, i0:i0 + 4], in0=ssq[:, i0:i0 + 4], scalar1=1.0 / D,
                                                   scalar2=1e-6, op0=ALU.mult, op1=ALU.add), s_p1["rs"])
                    cx.wait("pool", r1)
                    r2 = cx.inc(pool.tensor_tensor(out=rstd[:, i0:i0 + 4], in0=rtmp[:, i0:i0 + 4], in1=mhalf[:, 0:4],
                                                   op=ALU.pow), s_p1["rs"])
                    a1_tk[(s, b)] = (r2, ldxs)

                def stageA2(s, b):
                    r2, ldxs = a1_tk[(s, b)]
                    for ii in range(4):
                        i = b * 4 + ii
                        u = s * NT + i
                        cx.wait("dve", r2)
                        cx.wait("dve", rel_hb.get(u - NH))
                        cx.wait("dve", ldxs[ii])
                        htk = cx.inc(dve.scalar_tensor_tensor(out=hb[u % NH][:], in0=xt[u % NX][:], scalar=rstd[:, i:i + 1],
                                                              in1=g_rep[:], op0=ALU.mult, op1=ALU.mult), s_p1["h"])
                        rel_xt[u] = htk
                        h_tk[u] = htk

                def stageB(s, b):
                    for ii in range(4):
                        i = b * 4 + ii
                        u = s * NT + i
                        cx.wait("pe", h_tk[u])
                        cx.wait("pe", rel_tr.get(u - 2))
                        trb = bank(u % 2).bitcast(BF16)
                        for c in range(8):
                            ins = pe.transpose(out=trb[:, c * 128:(c + 1) * 128], in_=hb[u % NH][:, c * 128:(c + 1) * 128],
                                               identity=ident[:])
                        trk = cx.inc(ins, s_p1["tr"])
                        rel_hb[u] = trk
                        cx.wait("act", trk)
                        if ii == 0:
                            cx.wait("act", rel_hT.get((s * 8 + b) - 2))
                        hte = cx.inc(act.activation(out=hT[b % 2][:, :, ii * 128:(ii + 1) * 128],
                                                    in_=trb.rearrange("p (c t) -> p c t", c=8), func=AF.Copy), s_p1["hte"])
                        rel_tr[u] = hte
                        hte_last[(s, b)] = hte

                def main1(s, b, mid=None, mid_f=None):
                    cx.wait("pe", hte_last[(s, b)])
                    hTb = hT[b % 2]
                    col_bases = [0, 512, 2048, 2560]
                    dsts = [QTa[s], KTa[s], QTb[s], KTb[s]]
                    for gi in range(4):
                        nSF = cnt["nSF"]
                        sf = nSF % 3
                        cx.wait("pe", w1_tk[col_bases[gi]])
                        for cb in range(4):
                            nF = cnt["nF"]
                            col0 = col_bases[gi] + cb * 128
                            pb = 2 + (nF % 3)
                            cx.wait("pe", rel_psF.get(nF - 3))
                            for c in range(8):
                                ins = pe.matmul(bank(pb), lhsT=W1[:, c, col0:col0 + 128], rhs=hTb[:, c, :],
                                                start=(c == 0), stop=(c == 7))
                            fmm = cx.inc(ins, s_p1["fmm"])
                            scale = 0.125 if gi in (0, 2) else 1.0
                            en = "dve" if (nF % 2 == 0) else "act"
                            cx.wait(en, fmm)
                            cx.wait(en, rel_stF.get(nSF - 3))
                            if en == "dve":
                                ins = dve.tensor_scalar(out=stgF[sf][:, cb, :], in0=bank(pb), scalar1=scale, scalar2=None,
                                                        op0=ALU.mult)
                            else:
                                ins = act.activation(out=stgF[sf][:, cb, :], in_=bank(pb), func=AF.Copy, scale=scale)
                            fev = cx.inc(ins, s_p1["fev" + en])
                            rel_psF[nF] = fev
                            cnt["nF"] += 1
                        cx.wait("sp", (s_p1["fevdve"], s_p1["fevdve"].n))
                        cx.wait("sp", (s_p1["fevact"], s_p1["fevact"].n))
                        stk = cx.dinc(sp.dma_start(out=dsts[gi][:, :, b * 512:(b + 1) * 512], in_=stgF[sf][:]),
                                      s_p1["stF"][sf])
                        rel_stF[nSF] = stk
                        cnt["nSF"] += 1
                        if gi == 1 and mid_f is not None:
                            mid_f()
                    if mid is not None:
                        mid()
                    for ii in range(4):
                        i = b * 4 + ii
                        tok0l = i * 128
                        nSV = cnt["nSV"]
                        sv = nSV % 2
                        cx.wait("dve", rel_stV.get(nSV - 2))
                        cx.wait("act", rel_stV.get(nSV - 2))
                        for gq, col0 in enumerate((1024, 1536, 3072, 3584)):
                            nT = cnt["nT"]
                            pb = 5 + (nT % 3)
                            cx.wait("pe", w1_tk[col0])
                            cx.wait("pe", rel_psT.get(nT - 3))
                            for c in range(8):
                                ins = pe.matmul(bank(pb), lhsT=hTb[:, c, ii * 128:(ii + 1) * 128],
                                                rhs=W1[:, c, col0:col0 + 512], start=(c == 0), stop=(c == 7))
                            tmm = cx.inc(ins, s_p1["tmm"])
                            if gq == 0:
                                cx.wait("dve", tmm)
                                ins = dve.tensor_copy(out=stgVa[sv][:, :, 0:64],
                                                      in_=bank(pb).rearrange("p (h e) -> p h e", h=8))
                            elif gq == 2:
                                cx.wait("dve", tmm)
                                ins = dve.tensor_copy(out=stgVb[sv][:, :, 0:128],
                                                      in_=bank(pb).rearrange("p (h e) -> p h e", h=4))
                            else:
                                cx.wait("act", tmm)
                                o0 = 0 if gq == 1 else 512
                                ins = act.activation(out=stgG[sv][:, o0:o0 + 512], in_=bank(pb), func=AF.Silu)
                            tev = cx.inc(ins, s_p1["tevdve" if gq in (0, 2) else "tevact"])
                            rel_psT[nT] = tev
                            cnt["nT"] += 1
                        if ii == 3:
                            rel_hT[s * 8 + b] = tmm
                        cx.wait("sp", (s_p1["tevdve"], s_p1["tevdve"].n))
                        cx.wait("sp", (s_p1["tevact"], s_p1["tevact"].n))
                        sp.dma_start(out=Va[s][tok0l:tok0l + 128, :], in_=stgVa[sv][:].rearrange("p h e -> p (h e)")
                                     ).then_inc(s_p1["stV"][sv].h, 16)
                        sp.dma_start(out=Vb[s][tok0l:tok0l + 128, :], in_=stgVb[sv][:].rearrange("p h e -> p (h e)")
                                     ).then_inc(s_p1["stV"][sv].h, 16)
                        s_p1["stV"][sv].n += 32
                        stk = cx.dinc(sp.dma_start(out=Gs[s][tok0l:tok0l + 128, :], in_=stgG[sv][:]), s_p1["stV"][sv])
                        rel_stV[nSV] = stk
                        cnt["nSV"] += 1

                blocks = [(s, b) for s in range(nseq) for b in range(S // 512)]
                stageL(*blocks[0])
                for c in range(8):
                    cx.dinc(pool.dma_start(out=W1[:, c, :], in_=w_in[l, c * 128:(c + 1) * 128, :]), s_w1)
                w1_all = (s_w1, s_w1.n)
                w1_tk = {col0: w1_all for col0 in (0, 512, 2048, 2560, 1024, 1536, 3072, 3584)}
                stageA(*blocks[0])
                stageA2(*blocks[0])
                stageB(*blocks[0])
                for k, (s, b) in enumerate(blocks):
                    if k + 1 < len(blocks):
                        nb = blocks[k + 1]
                        stageA(*nb)
                        main1(s, b, mid=lambda nb=nb: stageB(*nb), mid_f=lambda nb=nb: stageA2(*nb))
                    else:
                        main1(s, b)
                fin = [(sm, sm.n) for sm in s_p1["stF"] + s_p1["stV"]]
                for en in ("pe", "act", "dve", "pool", "sp"):
                    cx.wait(en, fin)
                barrier()

            if debug == "p1":
                break
            if l == 0:
                s2 = {k: cx.sem("s2" + k) for k in
                      ("cb", "kv", "qk", "ex", "pv", "post", "postp", "gg",
                       "aqk", "aex", "apv", "apost", "tr3", "ev3", "mm3", "add3", "sq3", "rs3", "of3")}
                for k, n in (("ldq", 2), ("ldg", 2), ("yst", 2), ("ldqa", 4), ("ldga", 4), ("ysta", 2),
                             ("ldy", 2), ("ldx3", 2), ("st3", 2)):
                    s2[k] = [cx.sem(f"s2{k}{i}") for i in range(n)]
                cx.s2 = s2
            s2 = cx.s2
            with ExitStack() as p2:
                Wo = sb("Wo", [128, 8, D], BF16, p2)
                Cb = sb("Cb", [128, NCOMBO, 1024], BF16, p2)
                for c in range(8):
                    cx.dinc(pool.dma_start(out=Wo[:, c, :], in_=w_out[l, c * 128:(c + 1) * 128, :]), s_wo)
                with ExitStack() as pr:
                    Rb = sb("Rb", [128, 7, 1024], BF16, pr)
                    cx.dinc(pool.dma_start(out=Rb[:], in_=na_R[l]), s_wo)
                    wo_tk = (s_wo, s_wo.n)
                    cx.wait("pool", wo_tk)
                    for ci, (oi, _m) in enumerate(_MASKS):
                        cb_tk = cx.inc(pool.tensor_tensor(
                            out=Cb[:, ci, :].rearrange("p (h q) -> p h q", h=8),
                            in0=Rb[:, oi, :].rearrange("p (h q) -> p h q", h=8),
                            in1=maskb[:, ci:ci + 1, :].broadcast_to([128, 8, 128]), op=ALU.add), s2["cb"])
                    for en in ("pe", "act", "dve", "pool", "sp"):
                        cx.wait(en, cb_tk)
                        cx.wait(en, wo_tk)
                    barrier()

                for s in range(nseq):
                    with ExitStack() as pb:
                        kT = sb("kT", [128, 4, S], BF16, pb)
                        vbt = sb("vbt", [128, NT, 516], BF16, pb)
                        qT = [sb(f"qT{i}", [128, 4, 256], BF16, pb) for i in range(2)]
                        GG = [sb(f"GG{i}", [128, 2, 512], F32, pb) for i in range(2)]
                        PT = [sb(f"PT{i}", [128, 1024], BF16, pb) for i in range(3)]
                        ybt = [sb(f"ybt{i}", [128, 2, 512], BF16, pb) for i in range(2)]
                        rr = sb("rr", [128, 4, 1], F32, pb)
                        cc = sb("cc", [128, 2, 1], F32, pb)
                        t1 = sb("t1", [128, 2, 128], F32, pb)
                        d0 = sb("d0", [128, 2, 128], F32, pb)
                        dd = sb("dd", [128, 2, 128], F32, pb)
                        sq = sb("sq", [128, 2, 128], F32, pb)
                        ssb = sb("ssb", [128, 2], F32, pb)
                        ttb = sb("ttb", [128, 2], F32, pb)
                        rsb = sb("rsb", [128, 2, 1], F32, pb)
                        tmpb = sb("tmpb", [128, 2, 128], F32, pb)
                        sgb2 = sgb[:].rearrange("p h e -> p (h e)")
                        for hh in range(4):
                            cx.dinc(sp.dma_start(out=kT[:, hh, :], in_=KTb[s][:, hh, :]), s2["kv"])
                        for q4 in range(4):
                            cx.dinc(sp.dma_start(
                                out=vbt[:, q4 * 8:(q4 + 1) * 8, :],
                                in_=Vb[s][q4 * 1024:(q4 + 1) * 1024, :].rearrange("(kt p) e -> p kt e", p=128)), s2["kv"])
                        kv_tk = (s2["kv"], s2["kv"].n)

                        NIB = S // 256
                        uoff = (l * nseq + s) * NIB
                        goff = (l * nseq + s) * (NIB * 4 * 16)
                        ld_tk = {}
                        gg_tk = {}
                        qk_tk = {}
                        ex_tk = {}
                        pv_tk = {}
                        accfree = {}
                        chain_last = {}
                        yst_tk = {}
                        lastqk_of_ib = {}

                        def load_q(ib):
                            sl = (uoff + ib) % 2
                            cx.wait("sp", lastqk_of_ib.get(ib - 2))
                            a = cx.dinc(sp.dma_start(out=qT[sl][:], in_=QTb[s][:, :, ib * 256:(ib + 1) * 256]), s2["ldq"][sl])
                            cx.wait("sp", chain_last.get((ib - 2, 3)))
                            b_ = cx.dinc(sp.dma_start(
                                out=GG[sl][:],
                                in_=Gs[s][ib * 256:(ib + 1) * 256, 512:1024].rearrange("(q p) f -> p q f", p=128)), s2["ldg"][sl])
                            ld_tk[ib] = (a, b_)

                        def make_gg(ib):
                            sl = (uoff + ib) % 2
                            cx.wait("pool", ld_tk[ib][1])
                            for qs in range(2):
                                tk_ = cx.inc(pool.tensor_tensor(out=GG[sl][:, qs, :], in0=GG[sl][:, qs, :], in1=sgb2, op=ALU.mult),
                                             s2["gg"])
                            gg_tk[ib] = tk_

                        groups = [(ib, hh, jp) for ib in range(NIB) for hh in range(4) for jp in range(16)]
                        NG = len(groups)

                        def emit_qk(g):
                            ib, hh, jp = groups[g]
                            gg_ = goff + g
                            sbi = gg_ % 2
                            sl = (uoff + ib) % 2
                            cx.wait("pe", kv_tk)
                            cx.wait("pe", ld_tk[ib][0])
                            cx.wait("pe", ex_tk.get(g - 2))
                            near = abs(jp - ib) <= 1
                            ins = None
                            for ktl in range(2):
                                kt = 2 * jp + ktl
                                for m in range(2):
                                    o0 = (2 * sbi + m) * 512 + ktl * 256
                                    ins = pe.matmul(ps[:, o0:o0 + 256], lhsT=kT[m * 64:(m + 1) * 64, hh, kt * 128:(kt + 1) * 128],
                                                    rhs=qT[sl][m * 64:(m + 1) * 64, hh, :], start=(ktl == 0), stop=(not near),
                                                    tile_position=(m * 64, 0), skip_group_check=True)
                            if near:
                                for ktl in range(2):
                                    kt = 2 * jp + ktl
                                    uu0 = 256 * ib - 128 * kt + 384
                                    for m in range(2):
                                        o0 = (2 * sbi + m) * 512 + ktl * 256
                                        ins = pe.matmul(ps[:, o0:o0 + 256], lhsT=ident[:], rhs=strip[:, hh, uu0:uu0 + 256],
                                                        start=False, stop=True, skip_group_check=True)
                            qk_tk[g] = cx.inc(ins, s2["qk"])
                            if hh == 3 and jp == 15:
                                lastqk_of_ib[ib] = qk_tk[g]

                        def emit_exp(g):
                            ib, hh, jp = groups[g]
                            gg_ = goff + g
                            sbi = gg_ % 2
                            cx.wait("act", qk_tk[g])
                            if jp < ib - 1:
                                bias = far[:, hh:hh + 1]
                            elif jp > ib + 1:
                                bias = far[:, 4 + hh:5 + hh]
                            else:
                                bias = zero1[:]
                            ex_tk[g] = cx.inc(act.activation(out=PT[gg_ % 3][:], in_=ps[:, sbi * 1024:(sbi + 1) * 1024], func=AF.Exp,
                                                             bias=bias, scale=1.0), s2["ex"])

                        def emit_pv(g):
                            ib, hh, jp = groups[g]
                            gg_ = goff + g
                            u = (uoff + ib) * 4 + hh
                            aset = u % 2
                            cx.wait("pe", ex_tk[g])
                            if jp == 0:
                                cx.wait("pe", accfree.get(u - 2))
                            ins = None
                            for ktl in range(2):
                                kt = 2 * jp + ktl
                                for m in range(2):
                                    for qs in range(2):
                                        o0 = (4 + 2 * aset + m) * 512 + qs * 129
                                        c0 = m * 512 + ktl * 256 + qs * 128
                                        ins = pe.matmul(ps[:, o0:o0 + 129], lhsT=PT[gg_ % 3][:, c0:c0 + 128],
                                                        rhs=vbt[:, kt, hh * 129:(hh + 1) * 129],
                                                        start=(jp == 0 and ktl == 0 and qs == 0),
                                                        stop=(jp == 15 and ktl == 1), skip_group_check=True)
                            pv_tk[g] = cx.inc(ins, s2["pv"])
                            if jp == 15:
                                emit_chain(ib, hh, u, aset, pv_tk[g])

                        def emit_chain(ib, hh, u, aset, acc_tk):
                            sl = (uoff + ib) % 2
                            accs = []
                            for m in range(2):
                                a3 = bank(4 + 2 * aset + m)[:, 0:258].rearrange("p (q e) -> p q e", q=2)
                                accs.append(a3)
                            cx.wait("dve", acc_tk)
                            cx.wait("dve", cx.prev_chain)
                            k0 = cx.inc(dve.reciprocal(out=rr[:, 0:2, :], in_=accs[0][:, :, 128:129]), s2["post"])
                            k1 = cx.inc(dve.reciprocal(out=rr[:, 2:4, :], in_=accs[1][:, :, 128:129]), s2["post"])
                            cx.wait("dve", k1)
                            k2 = cx.inc(dve.tensor_scalar(out=cc[:], in0=rr[:, 2:4, :], scalar1=lamw[:, 4:5], scalar2=None,
                                                          op0=ALU.mult), s2["post"])
                            cx.wait("dve", k2)
                            cx.inc(dve.tensor_tensor(out=t1[:], in0=accs[1][:, :, 0:128], in1=cc[:].broadcast_to([128, 2, 128]),
                                                     op=ALU.mult), s2["post"])
                            k3 = cx.inc(dve.tensor_tensor(out=d0[:], in0=accs[0][:, :, 0:128],
                                                          in1=rr[:, 0:2, :].broadcast_to([128, 2, 128]), op=ALU.mult), s2["post"])
                            accfree[u] = k3
                            cx.wait("dve", k3)
                            k4 = cx.inc(dve.tensor_tensor(out=dd[:], in0=d0[:], in1=t1[:], op=ALU.subtract), s2["post"])
                            cx.wait("dve", k4)
                            k5 = cx.inc(dve.tensor_tensor(out=sq[:], in0=dd[:], in1=dd[:], op=ALU.mult), s2["post"])
                            cx.wait("dve", k5)
                            k6 = cx.inc(dve.reduce_sum(out=ssb[:], in_=sq[:], axis=AX.X), s2["post"])
                            cx.wait("pool", k6)
                            p1_ = cx.inc(pool.tensor_scalar(out=ttb[:], in0=ssb[:], scalar1=1.0 / 128, scalar2=1e-5,
                                                            op0=ALU.mult, op1=ALU.add), s2["postp"])
                            cx.wait("pool", p1_)
                            p2_ = cx.inc(pool.tensor_tensor(out=rsb[:, :, 0], in0=ttb[:], in1=mhalf[:, 0:2], op=ALU.pow), s2["postp"])
                            cx.wait("dve", p2_)
                            k7 = cx.inc(dve.tensor_tensor(out=tmpb[:], in0=dd[:], in1=rsb[:].broadcast_to([128, 2, 128]), op=ALU.mult),
                                        s2["post"])
                            cx.wait("dve", k7)
                            cx.wait("dve", gg_tk[ib])
                            if hh == 0:
                                cx.wait("dve", yst_tk.get(ib - 2))
                            k8 = cx.inc(dve.tensor_tensor(out=ybt[sl][:, :, hh * 128:(hh + 1) * 128], in0=tmpb[:],
                                                          in1=GG[sl][:, :, hh * 128:(hh + 1) * 128], op=ALU.mult), s2["post"])
                            chain_last[(ib, hh)] = k8
                            cx.prev_chain = k8
                            if hh == 3:
                                cx.wait("sp", k8)
                                yst_tk[ib] = cx.dinc(sp.dma_start(
                                    out=Ys[s][ib * 256:(ib + 1) * 256, 512:1024].rearrange("(q p) f -> p q f", p=128),
                                    in_=ybt[sl][:]), s2["yst"][sl])
                                if ib + 2 < NIB:
                                    load_q(ib + 2)
                                if ib + 1 < NIB:
                                    make_gg(ib + 1)

                        load_q(0)
                        load_q(1)
                        make_gg(0)
                        emit_qk(0)
                        emit_qk(1)
                        for g in range(NG):
                            emit_exp(g)
                            if g + 2 < NG:
                                emit_qk(g + 2)
                            emit_pv(g)
                        fin = [(sm, sm.n) for sm in s2["yst"]]
                        for en in ("pe", "act", "dve", "pool", "sp"):
                            cx.wait(en, fin)
                        barrier()
                    if debug == "p2b":
                        continue

                    with ExitStack() as pa:
                        kT = sb("kTa", [128, 4, S], BF16, pa)
                        vat = sb("vat", [128, NT, 520], BF16, pa)
                        qa = [sb(f"qa{i}", [128, 4, 128], BF16, pa) for i in range(4)]
                        ga = [sb(f"ga{i}", [128, 512], F32, pa) for i in range(4)]
                        PT = [sb(f"PTa{i}", [128, 1024], BF16, pa) for i in range(3)]
                        yat = [sb(f"yat{i}", [128, 512], BF16, pa) for i in range(2)]
                        rrA = sb("rrA", [128, 8, 1], F32, pa)
                        tmpA = sb("tmpA", [128, 8, 64], F32, pa)
                        for hh in range(4):
                            cx.dinc(sp.dma_start(out=kT[:, hh, :], in_=KTa[s][:, hh, :]), s2["kv"])
                        for q4 in range(4):
                            cx.dinc(sp.dma_start(
                                out=vat[:, q4 * 8:(q4 + 1) * 8, :],
                                in_=Va[s][q4 * 1024:(q4 + 1) * 1024, :].rearrange("(kt p) e -> p kt e", p=128)), s2["kv"])
                        kv_tk = (s2["kv"], s2["kv"].n)
                        toff = (l * nseq + s) * NT
                        groups = [(t, kt) for t in range(NT) for kt in _na_kts(t)]
                        NG = len(groups)
                        goff = (l * nseq + s) * NG
                        ld_tk = {}
                        qk_tk = {}
                        ex_tk = {}
                        pv_tk = {}
                        accfree = {}
                        yst_tk = {}
                        lastqk_of_t = {}
                        chain_last = {}

                        def load_qa(t):
                            sl4 = (toff + t) % 4
                            cx.wait("sp", lastqk_of_t.get(t - 4))
                            a = cx.dinc(sp.dma_start(out=qa[sl4][:], in_=QTa[s][:, :, t * 128:(t + 1) * 128]), s2["ldqa"][sl4])
                            cx.wait("sp", chain_last.get(t - 4))
                            b_ = cx.dinc(sp.dma_start(out=ga[sl4][:], in_=Gs[s][t * 128:(t + 1) * 128, 0:512]), s2["ldga"][sl4])
                            ld_tk[t] = (a, b_)

                        def emit_qk_a(g):
                            t, kt = groups[g]
                            gg_ = goff + g
                            sbi = gg_ % 2
                            sl = (toff + t) % 4
                            cx.wait("pe", kv_tk)
                            cx.wait("pe", ld_tk[t][0])
                            cx.wait("pe", ex_tk.get(g - 2))
                            for j in range(4):
                                for par in range(2):
                                    o0 = (2 * sbi + par) * 512 + j * 128
                                    pe.matmul(ps[:, o0:o0 + 128], lhsT=kT[par * 64:(par + 1) * 64, j, kt * 128:(kt + 1) * 128],
                                              rhs=qa[sl][par * 64:(par + 1) * 64, j, :], start=(j == 0), stop=False,
                                              tile_position=(par * 64, 0), skip_group_check=True)
                            ci = _COMBO[(t, kt)]
                            pe.matmul(bank(2 * sbi), lhsT=ident[:], rhs=Cb[:, ci, 0:512], start=False, stop=True, skip_group_check=True)
                            ins = pe.matmul(bank(2 * sbi + 1), lhsT=ident[:], rhs=Cb[:, ci, 512:1024], start=False, stop=True,
                                            skip_group_check=True)
                            qk_tk[g] = cx.inc(ins, s2["aqk"])
                            if kt == _na_kts(t)[-1]:
                                lastqk_of_t[t] = qk_tk[g]

                        def emit_exp_a(g):
                            gg_ = goff + g
                            sbi = gg_ % 2
                            cx.wait("act", qk_tk[g])
                            ex_tk[g] = cx.inc(act.activation(out=PT[gg_ % 3][:], in_=ps[:, sbi * 1024:(sbi + 1) * 1024], func=AF.Exp,
                                                             bias=zero1[:], scale=1.0), s2["aex"])

                        def emit_pv_a(g):
                            t, kt = groups[g]
                            gg_ = goff + g
                            u = toff + t
                            aset = u % 2
                            kts = _na_kts(t)
                            cx.wait("pe", ex_tk[g])
                            if kt == kts[0]:
                                cx.wait("pe", accfree.get(t - 2))
                            ins = None
                            for hh in range(8):
                                o0 = (4 + 2 * aset + hh // 4) * 512 + (hh % 4) * 65
                                c0 = (hh % 2) * 512 + (hh // 2) * 128
                                ins = pe.matmul(ps[:, o0:o0 + 65], lhsT=PT[gg_ % 3][:, c0:c0 + 128], rhs=vat[:, kt, hh * 65:(hh + 1) * 65],
                                                start=(kt == kts[0] and hh % 4 == 0), stop=(kt == kts[-1]), skip_group_check=True)
                            pv_tk[g] = cx.inc(ins, s2["apv"])
                            if kt == kts[-1]:
                                emit_chain_a(t, aset, pv_tk[g])

                        def emit_chain_a(t, aset, acc_tk):
                            sl = (toff + t) % 2
                            sl4 = (toff + t) % 4
                            cx.wait("dve", acc_tk)
                            cx.wait("dve", cx.prev_chain_a)
                            accv = [bank(4 + 2 * aset + bk)[:, 0:260].rearrange("p (h e) -> p h e", h=4) for bk in range(2)]
                            cx.inc(dve.reciprocal(out=rrA[:, 0:4, :], in_=accv[0][:, :, 64:65]), s2["apost"])
                            k1 = cx.inc(dve.reciprocal(out=rrA[:, 4:8, :], in_=accv[1][:, :, 64:65]), s2["apost"])
                            cx.wait("dve", k1)
                            cx.inc(dve.tensor_tensor(out=tmpA[:, 0:4, :], in0=accv[0][:, :, 0:64],
                                                     in1=rrA[:, 0:4, :].broadcast_to([128, 4, 64]), op=ALU.mult), s2["apost"])
                            k2 = cx.inc(dve.tensor_tensor(out=tmpA[:, 4:8, :], in0=accv[1][:, :, 0:64],
                                                          in1=rrA[:, 4:8, :].broadcast_to([128, 4, 64]), op=ALU.mult), s2["apost"])
                            accfree[t] = k2
                            cx.wait("dve", k2)
                            cx.wait("dve", ld_tk[t][1])
                            cx.wait("dve", yst_tk.get(t - 2))
                            k3 = cx.inc(dve.tensor_tensor(out=yat[sl][:], in0=tmpA[:].rearrange("p h e -> p (h e)"), in1=ga[sl4][:],
                                                          op=ALU.mult), s2["apost"])
                            chain_last[t] = k3
                            cx.prev_chain_a = k3
                            cx.wait("sp", k3)
                            yst_tk[t] = cx.dinc(sp.dma_start(out=Ys[s][t * 128:(t + 1) * 128, 0:512], in_=yat[sl][:]), s2["ysta"][sl])
                            if t + 3 < NT:
                                load_qa(t + 3)

                        load_qa(0)
                        load_qa(1)
                        load_qa(2)
                        emit_qk_a(0)
                        emit_qk_a(1)
                        for g in range(NG):
                            emit_exp_a(g)
                            if g + 2 < NG:
                                emit_qk_a(g + 2)
                            emit_pv_a(g)
                        fin = [(sm, sm.n) for sm in s2["ysta"]]
                        for en in ("pe", "act", "dve", "pool", "sp"):
                            cx.wait(en, fin)
                        barrier()
                    if debug == "p2a":
                        continue

                    with ExitStack() as p3:
                        yt = [sb(f"yt{i}", [128, D], BF16, p3) for i in range(2)]
                        x3 = [sb(f"x3{i}", [128, D], F32, p3) for i in range(2)]
                        yT = [sb(f"yT{i}", [128, 8, 128], BF16, p3) for i in range(2)]
                        xo = [sb(f"xo{i}", [128, D], F32, p3) for i in range(2)]
                        of = [sb(f"of{i}", [128, D], F32, p3) for i in range(2)]
                        ss3 = sb("ss3", [128, NT], F32, p3)
                        tt3 = sb("tt3", [128, NT], F32, p3)
                        rs3 = sb("rs3", [128, NT], F32, p3)
                        toff = (l * nseq + s) * NT
                        ev_tk = {}
                        mm_tk = {}
                        add_tk = {}
                        st_tk = {}
                        tr_tk = {}
                        dst = out if (last_layer and final) else xs
                        def pre3(t):
                            u = toff + t
                            sl = u % 2
                            tok0 = s * S + t * 128
                            cx.wait("sp", tr_tk.get(t - 2))
                            ldy = cx.dinc(sp.dma_start(out=yt[sl][:], in_=Ys[s][t * 128:(t + 1) * 128, :]), s2["ldy"][sl])
                            cx.wait("sp", add_tk.get(t - 2))
                            ldx_tk[t] = cx.dinc(sp.dma_start(out=x3[sl][:], in_=xsrc[tok0:tok0 + 128, :]), s2["ldx3"][sl])
                            cx.wait("pe", ldy)
                            cx.wait("pe", ev_tk.get(t - 2))
                            trb = bank(sl).bitcast(BF16)
                            for c in range(8):
                                ins = pe.transpose(out=trb[:, c * 128:(c + 1) * 128], in_=yt[sl][:, c * 128:(c + 1) * 128],
                                                   identity=ident[:])
                            tr_tk[t] = cx.inc(ins, s2["tr3"])
                            cx.wait("act", tr_tk[t])
                            cx.wait("act", mm_tk.get(t - 2))
                            ev_tk[t] = cx.inc(act.activation(out=yT[sl][:].rearrange("p c t -> p (c t)"), in_=trb, func=AF.Copy),
                                              s2["ev3"])

                        def main3(t):
                            u = toff + t
                            sl = u % 2
                            tok0 = s * S + t * 128
                            cx.wait("pe", ev_tk[t])
                            cx.wait("pe", add_tk.get(t - 2))
                            for half in range(2):
                                for c in range(8):
                                    ins = pe.matmul(bank(2 + 2 * sl + half), lhsT=yT[sl][:, c, :], rhs=Wo[:, c, half * 512:(half + 1) * 512],
                                                    start=(c == 0), stop=(c == 7))
                            mm_tk[t] = cx.inc(ins, s2["mm3"])
                            cx.wait("dve", mm_tk[t])
                            cx.wait("dve", ldx_tk[t])
                            cx.wait("dve", st_tk.get(t - 2))
                            add_tk[t] = cx.inc(dve.tensor_tensor(out=xo[sl][:], in0=bank(2 + 2 * sl, 2), in1=x3[sl][:], op=ALU.add),
                                               s2["add3"])
                            if last_layer and final:
                                cx.wait("act", add_tk[t])
                                cx.wait("act", cx.junk_tk)
                                q1 = cx.inc(act.activation(out=junk[:], in_=xo[sl][:], func=AF.Square, accum_out=ss3[:, t:t + 1]),
                                            s2["sq3"])
                                cx.junk_tk = q1
                                cx.wait("pool", q1)
                                q2 = cx.inc(pool.tensor_scalar(out=tt3[:, t:t + 1], in0=ss3[:, t:t + 1], scalar1=1.0 / D, scalar2=1e-6,
                                                               op0=ALU.mult, op1=ALU.add), s2["rs3"])
                                cx.wait("pool", q2)
                                q3 = cx.inc(pool.tensor_tensor(out=rs3[:, t:t + 1], in0=tt3[:, t:t + 1], in1=mhalf[:, 0:1], op=ALU.pow),
                                            s2["rs3"])
                                cx.wait("dve", q3)
                                q4 = cx.inc(dve.scalar_tensor_tensor(out=of[sl][:], in0=xo[sl][:], scalar=rs3[:, t:t + 1], in1=fg_rep[:],
                                                                     op0=ALU.mult, op1=ALU.mult), s2["of3"])
                                cx.wait("pool", q4)
                                st_tk[t] = cx.dinc(pool.dma_start(out=dst[tok0:tok0 + 128, :], in_=of[sl][:]), s2["st3"][sl])
                            else:
                                cx.wait("pool", add_tk[t])
                                st_tk[t] = cx.dinc(pool.dma_start(out=dst[tok0:tok0 + 128, :], in_=xo[sl][:]), s2["st3"][sl])

                        ldx_tk = {}
                        pre3(0)
                        for t in range(NT):
                            if t + 1 < NT:
                                pre3(t + 1)
                            main3(t)
                        fin = [(sm, sm.n) for sm in s2["st3"]]
                        for en in ("pe", "act", "dve", "pool", "sp"):
                            cx.wait(en, fin)
                        barrier()
            xsrc = xs

        barrier()
    return nc


def _prep_shared(inputs):
    f32 = np.float32
    t5 = np.asarray(inputs["t5_table"], f32)
    bidx = _t5_strip_idx()
    strip = np.ascontiguousarray(t5[bidx].transpose(0, 2, 1))
    far = np.concatenate([t5[15], t5[31]])[None, :].astype(f32)
    rpb = np.asarray(inputs["na_rpb"], f32)
    nl = rpb.shape[0]
    rpb_ext = np.concatenate([rpb.reshape(nl, -1), np.zeros((nl, 1), f32)], axis=1)
    na_R = np.ascontiguousarray(rpb_ext[:, _RIDX].transpose(0, 2, 1, 3))
    na_mask = np.ascontiguousarray(np.stack([m for (_, m) in _MASKS], axis=1))
    lamv = np.concatenate([np.asarray(inputs[k], f32) for k in ("lambda_q1", "lambda_k1", "lambda_q2", "lambda_k2")], axis=1)
    return {
        "w_in": np.ascontiguousarray(inputs["w_in"], f32),
        "w_out": np.ascontiguousarray(inputs["w_out"], f32),
        "norm_g": np.ascontiguousarray(inputs["norm_g"], f32),
        "final_g": np.ascontiguousarray(np.asarray(inputs["final_g"], f32)[None, :]),
        "subln_g": np.ascontiguousarray(inputs["subln_g"], f32),
        "lamv": np.ascontiguousarray(lamv),
        "t5_strip": strip.astype(f32),
        "t5_far": far,
        "na_R": na_R.astype(f32),
        "na_mask": na_mask.astype(f32),
        "ident": np.eye(128, dtype=f32),
    }


def kernel(**inputs):
    x = np.asarray(inputs["x"], np.float32)
    B = x.shape[0]
    per = B // NCORES
    shared = _prep_shared(inputs)
    nc = build_nc(nseq=per, nlayers=DEPTH)
    in_maps = []
    for c in range(NCORES):
        m = dict(shared)
        m["x"] = np.ascontiguousarray(x[c * per:(c + 1) * per].reshape(per * S, D))
        in_maps.append(m)
    res = run_bass_kernel_spmd(nc, in_maps, core_ids=list(range(NCORES)))
    outs = [np.asarray(r["out"], np.float32).reshape(per, S, D) for r in res.results]
    return np.concatenate(outs, axis=0)
```

```python
import math
from contextlib import ExitStack

import numpy as np
import concourse.bass as bass
import concourse.mybir as mybir
from concourse.bass_utils import run_bass_kernel_spmd

F32 = mybir.dt.float32
BF16 = mybir.dt.bfloat16
AF = mybir.ActivationFunctionType
ALU = mybir.AluOpType
AX = mybir.AxisListType

S = 4096
D = 1024
NT = S // 128
GW = 64
DEPTH = 4
NCORES = 8
NEG = -30000.0


def _t5_bucket(rel):
    n = 16
    ret = np.where(rel > 0, n, 0)
    a = np.abs(rel)
    small = a < 8
    af = np.maximum(a, 1).astype(np.float32)
    v = (np.log(af / np.float32(8)) / np.float32(math.log(16.0)) * np.float32(8)).astype(np.float32)
    large = np.minimum(8 + v.astype(np.int32), n - 1)
    return ret + np.where(small, a, large)


def _t5_strip_idx():
    kl = np.arange(128)[:, None]
    uu = np.arange(896)[None, :]
    return _t5_bucket(kl - (uu - 384))


def _na_kts(t):
    r0, r1 = 2 * t, 2 * t + 1
    rs0 = min(max(r0 - 4, 0), 56)
    rs1 = min(max(r1 - 4, 0), 56)
    lo, hi = min(rs0, rs1), max(rs0, rs1) + 7
    return list(range(lo // 2, hi // 2 + 1))


def _na_tables():
    kl = np.arange(128)
    krl, kc = kl // 64, kl % 64
    ql = np.arange(128)
    qrl, qc = ql // 64, ql % 64
    cs = np.clip(qc - 8, 0, GW - 16)
    colok = (kc[:, None] >= cs[None, :]) & (kc[:, None] < cs[None, :] + 16)
    dc = kc[:, None] - qc[None, :]
    ridx = np.full((7, 128, 8, 128), -1, np.int64)
    for oi in range(7):
        o = oi - 3
        dr = 2 * o + krl[:, None] - qrl[None, :]
        ok = (dr + 7 >= 0) & (dr + 7 <= 14) & (dc + 15 >= 0) & (dc + 15 <= 30)
        base = (dr + 7) * 31 + (dc + 15)
        for h in range(8):
            ridx[oi, :, h, :] = np.where(ok, h * 15 * 31 + base, -1)
    ridx = ridx.reshape(7, 128, 4, 2, 128).transpose(0, 1, 3, 2, 4).reshape(7, 128, 1024)
    masks = []
    combo_of = {}
    key2id = {}
    for t in range(NT):
        for kt in _na_kts(t):
            qr = 2 * t + qrl
            rs = np.clip(qr - 4, 0, 56)
            kr = 2 * kt + krl
            rowok = (kr[:, None] >= rs[None, :]) & (kr[:, None] < rs[None, :] + 8)
            m = np.where(rowok & colok, 0.0, NEG).astype(np.float32)
            key = (kt - t, m.tobytes())
            if key not in key2id:
                key2id[key] = len(masks)
                masks.append((kt - t + 3, m))
            combo_of[(t, kt)] = key2id[key]
    return ridx, masks, combo_of


_RIDX, _MASKS, _COMBO = _na_tables()
NCOMBO = len(_MASKS)


class Sem:
    def __init__(self, nc, stack, name):
        self.h = stack.enter_context(nc.semaphore(name))
        self.n = 0
        self.name = name


class Ctx:
    def __init__(self, nc, stack):
        self.nc = nc
        self.stack = stack
        self.waited = {}
        self.junk_tk = None
        self.prev_chain = None
        self.prev_chain_a = None
        self.nsem = 0
        self.eng = {"pe": nc.tensor, "act": nc.scalar, "dve": nc.vector, "pool": nc.gpsimd, "sp": nc.sync}

    def sem(self, name):
        self.nsem += 1
        return Sem(self.nc, self.stack, name)

    def inc(self, ins, sem, amt=1):
        ins.then_inc(sem.h, amt)
        sem.n += amt
        return (sem, sem.n)

    def dinc(self, ins, sem):
        return self.inc(ins, sem, 16)

    def wait(self, en, tk):
        if tk is None:
            return
        if isinstance(tk, list):
            for t in tk:
                self.wait(en, t)
            return
        sem, val = tk
        key = (en, sem.name)
        if self.waited.get(key, 0) >= val:
            return
        self.waited[key] = val
        self.eng[en].wait_ge(sem.h, val)


def build_nc(nseq=2, nlayers=DEPTH, final=True, debug=False):
    nc = bass.Bass("TRN2", target_bir_lowering=False)
    NTOK = nseq * S
    x_in = nc.dram_tensor("x", [NTOK, D], F32, kind="ExternalInput").ap()
    w_in = nc.dram_tensor("w_in", [nlayers, D, 4096], F32, kind="ExternalInput").ap()
    w_out = nc.dram_tensor("w_out", [nlayers, D, D], F32, kind="ExternalInput").ap()
    norm_g = nc.dram_tensor("norm_g", [nlayers, D], F32, kind="ExternalInput").ap()
    final_g = nc.dram_tensor("final_g", [1, D], F32, kind="ExternalInput").ap()
    subln_g = nc.dram_tensor("subln_g", [nlayers, 128], F32, kind="ExternalInput").ap()
    lamv = nc.dram_tensor("lamv", [nlayers, 4 * 64], F32, kind="ExternalInput").ap()
    t5_strip = nc.dram_tensor("t5_strip", [128, 4, 896], F32, kind="ExternalInput").ap()
    t5_far = nc.dram_tensor("t5_far", [1, 8], F32, kind="ExternalInput").ap()
    na_R = nc.dram_tensor("na_R", [nlayers, 128, 7, 1024], F32, kind="ExternalInput").ap()
    na_mask = nc.dram_tensor("na_mask", [128, NCOMBO, 128], F32, kind="ExternalInput").ap()
    ident_in = nc.dram_tensor("ident", [128, 128], F32, kind="ExternalInput").ap()
    out = nc.dram_tensor("out", [NTOK, D], F32, kind="ExternalOutput").ap()
    okind = "ExternalOutput" if debug else "Internal"
    xs = nc.dram_tensor("xs", [NTOK, D], F32, kind="Internal").ap()
    QTa = [nc.dram_tensor(f"QTa{s}", [128, 4, S], BF16, kind=okind).ap() for s in range(nseq)]
    KTa = [nc.dram_tensor(f"KTa{s}", [128, 4, S], BF16, kind=okind).ap() for s in range(nseq)]
    QTb = [nc.dram_tensor(f"QTb{s}", [128, 4, S], BF16, kind=okind).ap() for s in range(nseq)]
    KTb = [nc.dram_tensor(f"KTb{s}", [128, 4, S], BF16, kind=okind).ap() for s in range(nseq)]
    Va = [nc.dram_tensor(f"Va{s}", [S, 520], BF16, kind=okind).ap() for s in range(nseq)]
    Vb = [nc.dram_tensor(f"Vb{s}", [S, 516], BF16, kind=okind).ap() for s in range(nseq)]
    Gs = [nc.dram_tensor(f"G{s}", [S, D], F32, kind=okind).ap() for s in range(nseq)]
    Ys = [nc.dram_tensor(f"Y{s}", [S, D], BF16, kind=okind).ap() for s in range(nseq)]

    with ExitStack() as stack:
        cx = Ctx(nc, stack)
        E = cx.eng
        pe, act, dve, pool, sp = E["pe"], E["act"], E["dve"], E["pool"], E["sp"]

        uniq = [0]

        def sb(name, shape, dt, st=stack):
            uniq[0] += 1
            return st.enter_context(nc.sbuf_tensor(f"sb{uniq[0]}_{name}", shape, dt))

        ps = stack.enter_context(nc.psum_tensor("ps", [128, 4096], F32))
        stack.enter_context(nc.Block())

        def bank(b, n=1):
            return ps[:, b * 512:(b + n) * 512]

        ident = sb("ident", [128, 128], BF16)
        strip = sb("strip", [128, 4, 896], BF16)
        far = sb("far", [128, 8], F32)
        zero1 = sb("zero1", [128, 1], F32)
        maskb = sb("maskb", [128, NCOMBO, 128], BF16)
        fg_rep = sb("fg_rep", [128, D], F32)
        g_rep = sb("g_rep", [128, D], F32)
        sgb = sb("sgb", [128, 4, 128], F32)
        lamt = sb("lamt", [128, 4 * 64], F32)
        lamw = sb("lamw", [128, 8], F32)
        junk = sb("junk", [128, D], BF16)
        mhalf = sb("mhalf", [128, NT], F32)

        s_setup = cx.sem("setup")
        s_bar = cx.sem("bar")

        def barrier():
            for en in ("pe", "act", "dve", "pool", "sp"):
                cx.inc(E[en].drain(), s_bar)
            tk = (s_bar, s_bar.n)
            for en in ("pe", "act", "dve", "pool", "sp"):
                cx.wait(en, tk)

        s_setup_sw = cx.sem("setupsw")
        cx.dinc(pool.dma_start(out=ident[:], in_=ident_in), s_setup_sw)
        cx.dinc(pool.dma_start(out=strip[:], in_=t5_strip), s_setup_sw)
        cx.dinc(pool.dma_start(out=maskb[:], in_=na_mask), s_setup_sw)
        cx.dinc(sp.dma_start(out=far[:], in_=t5_far.partition_broadcast(128)), s_setup)
        cx.dinc(sp.dma_start(out=fg_rep[:], in_=final_g.partition_broadcast(128)), s_setup)
        setup_tk = [(s_setup, s_setup.n), (s_setup_sw, s_setup_sw.n)]
        s_ms = cx.sem("ms")
        dve.memset(mhalf[:], -0.5)
        ms_tk = cx.inc(dve.memset(zero1[:], 0.0), s_ms)
        for en in ("pe", "act", "dve", "pool", "sp"):
            cx.wait(en, setup_tk)
            cx.wait(en, ms_tk)

        s_lay = cx.sem("lay")
        s_layc = cx.sem("layc")
        s_w1 = cx.sem("w1")
        cx.s_w1g = [cx.sem(f"w1g{i}") for i in range(8)]
        s_wo = cx.sem("wo")

        xsrc = x_in
        for l in range(nlayers):
            lam_init = 0.8 - 0.6 * math.exp(-0.3 * l)
            last_layer = (l == nlayers - 1)
            cx.dinc(sp.dma_start(out=g_rep[:], in_=norm_g[l:l + 1, :].partition_broadcast(128)), s_lay)
            cx.dinc(sp.dma_start(out=lamt[:], in_=lamv[l:l + 1, :].partition_broadcast(128)), s_lay)
            tk = cx.dinc(sp.dma_start(out=sgb[:, 0, :], in_=subln_g[l:l + 1, :].partition_broadcast(128)), s_lay)
            cx.wait("dve", tk)
            cx.wait("dve", cx.junk_tk)
            t1 = cx.inc(dve.tensor_tensor(out=junk[:, 0:64], in0=lamt[:, 0:64], in1=lamt[:, 64:128], op=ALU.mult), s_layc)
            t2 = cx.inc(dve.tensor_tensor(out=junk[:, 64:128], in0=lamt[:, 128:192], in1=lamt[:, 192:256], op=ALU.mult), s_layc)
            cx.wait("dve", t2)
            t2b = cx.inc(dve.reduce_sum(out=lamw[:, 0:1], in_=junk[:, 0:64], axis=AX.X), s_layc)
            cx.wait("dve", t2b)
            t3 = cx.inc(dve.reduce_sum(out=lamw[:, 1:2], in_=junk[:, 64:128], axis=AX.X), s_layc)
            cx.junk_tk = t3
            cx.wait("act", t3)
            t4 = cx.inc(act.activation(out=lamw[:, 2:4], in_=lamw[:, 0:2], func=AF.Exp), s_layc)
            cx.wait("dve", t4)
            t5 = cx.inc(dve.tensor_tensor(out=lamw[:, 5:6], in0=lamw[:, 2:3], in1=lamw[:, 3:4], op=ALU.subtract), s_layc)
            cx.wait("dve", t5)
            t6 = cx.inc(dve.tensor_scalar(out=lamw[:, 4:5], in0=lamw[:, 5:6], scalar1=float(lam_init), scalar2=None, op0=ALU.add), s_layc)
            t7 = cx.inc(dve.tensor_scalar(out=sgb[:, 0, :], in0=sgb[:, 0, :], scalar1=float(1.0 - lam_init), scalar2=None, op0=ALU.mult), s_layc)
            cx.wait("dve", t7)
            for hh in range(1, 4):
                t8 = cx.inc(dve.tensor_copy(out=sgb[:, hh, :], in_=sgb[:, 0, :]), s_layc)
            lay_tk = [t6, t8, (s_lay, s_lay.n)]
            for en in ("pe", "act", "dve", "pool", "sp"):
                cx.wait(en, lay_tk)

            lay = ExitStack()
            Wo = sb("Wo", [128, 8, D], BF16, lay)
            Cb = sb("Cb", [128, NCOMBO, 1024], BF16, lay)
            Rb = sb("Rb", [128, 7, 1024], BF16, lay)
            if l == 0:
                cx.s_cb = cx.sem("laycb")
            with ExitStack() as p1:
                W1 = sb("W1", [128, 8, 4096], BF16, p1)
                xt = [sb(f"xt{i}", [128, D], F32, p1) for i in range(4)]
                hb = [sb(f"hb{i}", [128, D], BF16, p1) for i in range(4)]
                hT = [sb(f"hT{i}", [128, 8, 512], BF16, p1) for i in range(2)]
                stgF = [sb(f"stgF{i}", [128, 4, 512], BF16, p1) for i in range(3)]
                stgVa = [sb(f"stgVa{i}", [128, 8, 65], BF16, p1) for i in range(2)]
                stgVb = [sb(f"stgVb{i}", [128, 4, 129], BF16, p1) for i in range(2)]
                stgG = [sb(f"stgG{i}", [128, D], F32, p1) for i in range(2)]
                ssq = sb("ssq", [128, NT], F32, p1)
                rtmp = sb("rtmp", [128, NT], F32, p1)
                rstd = sb("rstd", [128, NT], F32, p1)
                if l == 0:
                    s_p1 = {k: cx.sem("p1" + k) for k in
                            ("ss", "rs", "h", "tr", "hte", "fmm", "tmm", "fevdve", "fevact", "tevdve", "tevact", "ones")}
                    s_p1["ldx"] = [cx.sem(f"p1ldx{i}") for i in range(4)]
                    s_p1["stF"] = [cx.sem(f"p1stF{i}") for i in range(3)]
                    s_p1["stV"] = [cx.sem(f"p1stV{i}") for i in range(2)]
                    cx.s_p1 = s_p1
                s_p1 = cx.s_p1
                for i in range(2):
                    pool.memset(stgVa[i][:, :, 64:65], 1.0)
                    ones_tk = cx.inc(pool.memset(stgVb[i][:, :, 128:129], 1.0), s_p1["ones"])
                cx.wait("dve", ones_tk)
                cx.wait("act", ones_tk)
                cx.wait("sp", ones_tk)

                rel_xt = {}
                rel_hb = {}
                rel_tr = {}
                rel_hT = {}
                rel_psF = {}
                rel_psT = {}
                rel_stF = {}
                rel_stV = {}
                cnt = {"nF": 0, "nT": 0, "nSF": 0, "nSV": 0}
                hte_last = {}
                h_tk = {}
                a1_tk = {}
                NX = len(xt)
                NH = len(hb)

                ldx_of = {}

                def stageL(s, b):
                    ldxs = {}
                    for ii in range(4):
                        i = b * 4 + ii
                        u = s * NT + i
                        tok0 = s * S + i * 128
                        cx.wait("pool", rel_xt.get(u - NX))
                        ldxs[ii] = cx.dinc(pool.dma_start(out=xt[u % NX][:], in_=xsrc[tok0:tok0 + 128, :]), s_p1["ldx"][u % NX])
                    ldx_of[(s, b)] = ldxs

                def stageA(s, b):
                    if (s, b) not in ldx_of:
                        stageL(s, b)
                    ldxs = ldx_of[(s, b)]
                    ssts = {}
                    for ii in range(4):
                        i = b * 4 + ii
                        u = s * NT + i
                        cx.wait("act", ldxs[ii])
                        cx.wait("act", cx.junk_tk)
                        ssts[ii] = cx.inc(act.activation(out=junk[:], in_=xt[u % NX][:], func=AF.Square,
                                                         accum_out=ssq[:, i:i + 1]), s_p1["ss"])
                        cx.junk_tk = ssts[ii]
                    i0 = b * 4
                    cx.wait("pool", ssts[3])
                    r1 = cx.inc(pool.tensor_scalar(out=rtmp[:, i0:i0 + 4], in0=ssq[:, i0:i0 + 4], scalar1=1.0 / D,
                                                   scalar2=1e-6, op0=ALU.mult, op1=ALU.add), s_p1["rs"])
                    cx.wait("pool", r1)
                    r2 = cx.inc(pool.tensor_tensor(out=rstd[:, i0:i0 + 4], in0=rtmp[:, i0:i0 + 4], in1=mhalf[:, 0:4],
                                                   op=ALU.pow), s_p1["rs"])
                    a1_tk[(s, b)] = (r2, ldxs)

                def stageA2(s, b):
                    r2, ldxs = a1_tk[(s, b)]
                    for ii in range(4):
                        i = b * 4 + ii
                        u = s * NT + i
                        cx.wait("dve", r2)
                        cx.wait("dve", rel_hb.get(u - NH))
                        cx.wait("dve", ldxs[ii])
                        htk = cx.inc(dve.scalar_tensor_tensor(out=hb[u % NH][:], in0=xt[u % NX][:], scalar=rstd[:, i:i + 1],
                                                              in1=g_rep[:], op0=ALU.mult, op1=ALU.mult), s_p1["h"])
                        rel_xt[u] = htk
                        h_tk[u] = htk

                def stageB(s, b):
                    for ii in range(4):
                        i = b * 4 + ii
                        u = s * NT + i
                        cx.wait("pe", h_tk[u])
                        cx.wait("pe", rel_tr.get(u - 2))
                        trb = bank(u % 2).bitcast(BF16)
                        for c in range(8):
                            ins = pe.transpose(out=trb[:, c * 128:(c + 1) * 128], in_=hb[u % NH][:, c * 128:(c + 1) * 128],
                                               identity=ident[:])
                        trk = cx.inc(ins, s_p1["tr"])
                        rel_hb[u] = trk
                        cx.wait("act", trk)
                        if ii == 0:
                            cx.wait("act", rel_hT.get((s * 8 + b) - 2))
                        hte = cx.inc(act.activation(out=hT[b % 2][:, :, ii * 128:(ii + 1) * 128],
                                                    in_=trb.rearrange("p (c t) -> p c t", c=8), func=AF.Copy), s_p1["hte"])
                        rel_tr[u] = hte
                        hte_last[(s, b)] = hte

                def main1(s, b, mid=None, mid_f=None):
                    cx.wait("pe", hte_last[(s, b)])
                    hTb = hT[b % 2]
                    col_bases = [0, 512, 2048, 2560]
                    dsts = [QTa[s], KTa[s], QTb[s], KTb[s]]
                    for gi in range(4):
                        nSF = cnt["nSF"]
                        sf = nSF % 3
                        cx.wait("pe", w1_tk[col_bases[gi]])
                        for cb in range(4):
                            nF = cnt["nF"]
                            col0 = col_bases[gi] + cb * 128
                            pb = 2 + (nF % 3)
                            cx.wait("pe", rel_psF.get(nF - 3))
                            for c in range(8):
                                ins = pe.matmul(bank(pb), lhsT=W1[:, c, col0:col0 + 128], rhs=hTb[:, c, :],
                                                start=(c == 0), stop=(c == 7))
                            fmm = cx.inc(ins, s_p1["fmm"])
                            scale = 0.125 if gi in (0, 2) else 1.0
                            en = "dve" if (nF % 2 == 0) else "act"
                            cx.wait(en, fmm)
                            cx.wait(en, rel_stF.get(nSF - 3))
                            if en == "dve":
                                ins = dve.tensor_scalar(out=stgF[sf][:, cb, :], in0=bank(pb), scalar1=scale, scalar2=None,
                                                        op0=ALU.mult)
                            else:
                                ins = act.activation(out=stgF[sf][:, cb, :], in_=bank(pb), func=AF.Copy, scale=scale)
                            fev = cx.inc(ins, s_p1["fev" + en])
                            rel_psF[nF] = fev
                            cnt["nF"] += 1
                        cx.wait("sp", (s_p1["fevdve"], s_p1["fevdve"].n))
                        cx.wait("sp", (s_p1["fevact"], s_p1["fevact"].n))
                        stk = cx.dinc(sp.dma_start(out=dsts[gi][:, :, b * 512:(b + 1) * 512], in_=stgF[sf][:]),
                                      s_p1["stF"][sf])
                        rel_stF[nSF] = stk
                        cnt["nSF"] += 1
                        if gi == 1 and mid_f is not None:
                            mid_f()
                    if mid is not None:
                        mid()
                    for ii in range(4):
                        i = b * 4 + ii
                        tok0l = i * 128
                        nSV = cnt["nSV"]
                        sv = nSV % 2
                        cx.wait("dve", rel_stV.get(nSV - 2))
                        cx.wait("act", rel_stV.get(nSV - 2))
                        for gq, col0 in enumerate((1024, 1536, 3072, 3584)):
                            nT = cnt["nT"]
                            pb = 5 + (nT % 3)
                            cx.wait("pe", w1_tk[col0])
                            cx.wait("pe", rel_psT.get(nT - 3))
                            for c in range(8):
                                ins = pe.matmul(bank(pb), lhsT=hTb[:, c, ii * 128:(ii + 1) * 128],
                                                rhs=W1[:, c, col0:col0 + 512], start=(c == 0), stop=(c == 7))
                            tmm = cx.inc(ins, s_p1["tmm"])
                            if gq == 0:
                                cx.wait("dve", tmm)
                                ins = dve.tensor_copy(out=stgVa[sv][:, :, 0:64],
                                                      in_=bank(pb).rearrange("p (h e) -> p h e", h=8))
                            elif gq == 2:
                                cx.wait("dve", tmm)
                                ins = dve.tensor_copy(out=stgVb[sv][:, :, 0:128],
                                                      in_=bank(pb).rearrange("p (h e) -> p h e", h=4))
                            else:
                                cx.wait("act", tmm)
                                o0 = 0 if gq == 1 else 512
                                ins = act.activation(out=stgG[sv][:, o0:o0 + 512], in_=bank(pb), func=AF.Silu)
                            tev = cx.inc(ins, s_p1["tevdve" if gq in (0, 2) else "tevact"])
                            rel_psT[nT] = tev
                            cnt["nT"] += 1
                        if ii == 3:
                            rel_hT[s * 8 + b] = tmm
                        cx.wait("sp", (s_p1["tevdve"], s_p1["tevdve"].n))
                        cx.wait("sp", (s_p1["tevact"], s_p1["tevact"].n))
                        sp.dma_start(out=Va[s][tok0l:tok0l + 128, :], in_=stgVa[sv][:].rearrange("p h e -> p (h e)")
                                     ).then_inc(s_p1["stV"][sv].h, 16)
                        sp.dma_start(out=Vb[s][tok0l:tok0l + 128, :], in_=stgVb[sv][:].rearrange("p h e -> p (h e)")
                                     ).then_inc(s_p1["stV"][sv].h, 16)
                        s_p1["stV"][sv].n += 32
                        stk = cx.dinc(sp.dma_start(out=Gs[s][tok0l:tok0l + 128, :], in_=stgG[sv][:]), s_p1["stV"][sv])
                        rel_stV[nSV] = stk
                        cnt["nSV"] += 1

                blocks = [(s, b) for s in range(nseq) for b in range(S // 512)]
                stageL(*blocks[0])
                for c in range(8):
                    cx.dinc(pool.dma_start(out=W1[:, c, :], in_=w_in[l, c * 128:(c + 1) * 128, :]), s_w1)
                w1_all = (s_w1, s_w1.n)
                for c in range(8):
                    cx.dinc(pool.dma_start(out=Wo[:, c, :], in_=w_out[l, c * 128:(c + 1) * 128, :]), s_wo)
                cx.dinc(pool.dma_start(out=Rb[:], in_=na_R[l]), s_wo)
                wo_tk = (s_wo, s_wo.n)

                def build_cb():
                    cx.wait("pool", wo_tk)
                    for ci, (oi, _m) in enumerate(_MASKS):
                        cx.cb_tk = cx.inc(pool.tensor_tensor(
                            out=Cb[:, ci, :].rearrange("p (h q) -> p h q", h=8),
                            in0=Rb[:, oi, :].rearrange("p (h q) -> p h q", h=8),
                            in1=maskb[:, ci:ci + 1, :].broadcast_to([128, 8, 128]), op=ALU.add), cx.s_cb)
                w1_tk = {col0: w1_all for col0 in (0, 512, 2048, 2560, 1024, 1536, 3072, 3584)}
                stageA(*blocks[0])
                stageA2(*blocks[0])
                stageB(*blocks[0])
                for k, (s, b) in enumerate(blocks):
                    if k == 3:
                        build_cb()
                    if k + 1 < len(blocks):
                        nb = blocks[k + 1]
                        stageA(*nb)
                        main1(s, b, mid=lambda nb=nb: stageB(*nb), mid_f=lambda nb=nb: stageA2(*nb))
                    else:
                        main1(s, b)
                fin = [(sm, sm.n) for sm in s_p1["stF"] + s_p1["stV"]]
                for en in ("pe", "act", "dve", "pool", "sp"):
                    cx.wait(en, fin)
                barrier()

            if debug == "p1":
                break
            if l == 0:
                s2 = {k: cx.sem("s2" + k) for k in
                      ("cb", "kv", "qk", "ex", "pv", "post", "postp", "gg",
                       "aqk", "aex", "apv", "apost", "tr3", "ev3", "mm3", "add3", "sq3", "rs3", "of3")}
                for k, n in (("ldq", 2), ("ldg", 2), ("yst", 2), ("ldqa", 4), ("ldga", 4), ("ysta", 2),
                             ("ldy", 2), ("ldx3", 2), ("st3", 2)):
                    s2[k] = [cx.sem(f"s2{k}{i}") for i in range(n)]
                cx.s2 = s2
            s2 = cx.s2
            with ExitStack() as p2:
                p2.enter_context(lay)
                for en in ("pe", "act", "dve", "pool", "sp"):
                    cx.wait(en, cx.cb_tk)
                    cx.wait(en, wo_tk)

                for s in range(nseq):
                    with ExitStack() as pb:
                        kT = sb("kT", [128, 4, S], BF16, pb)
                        vbt = sb("vbt", [128, NT, 516], BF16, pb)
                        qT = [sb(f"qT{i}", [128, 4, 256], BF16, pb) for i in range(2)]
                        GG = [sb(f"GG{i}", [128, 2, 512], F32, pb) for i in range(2)]
                        PT = [sb(f"PT{i}", [128, 1024], BF16, pb) for i in range(3)]
                        ybt = [sb(f"ybt{i}", [128, 2, 512], BF16, pb) for i in range(2)]
                        rr = sb("rr", [128, 4, 1], F32, pb)
                        cc = sb("cc", [128, 2, 1], F32, pb)
                        t1 = sb("t1", [128, 2, 128], F32, pb)
                        d0 = sb("d0", [128, 2, 128], F32, pb)
                        dd = sb("dd", [128, 2, 128], F32, pb)
                        sq = sb("sq", [128, 2, 128], F32, pb)
                        ssb = sb("ssb", [128, 2], F32, pb)
                        ttb = sb("ttb", [128, 2], F32, pb)
                        rsb = sb("rsb", [128, 2, 1], F32, pb)
                        tmpb = sb("tmpb", [128, 2, 128], F32, pb)
                        sgb2 = sgb[:].rearrange("p h e -> p (h e)")
                        for hh in range(4):
                            cx.dinc(sp.dma_start(out=kT[:, hh, :], in_=KTb[s][:, hh, :]), s2["kv"])
                        for q4 in range(4):
                            cx.dinc(sp.dma_start(
                                out=vbt[:, q4 * 8:(q4 + 1) * 8, :],
                                in_=Vb[s][q4 * 1024:(q4 + 1) * 1024, :].rearrange("(kt p) e -> p kt e", p=128)), s2["kv"])
                        kv_tk = (s2["kv"], s2["kv"].n)

                        NIB = S // 256
                        uoff = (l * nseq + s) * NIB
                        goff = (l * nseq + s) * (NIB * 4 * 16)
                        ld_tk = {}
                        gg_tk = {}
                        qk_tk = {}
                        ex_tk = {}
                        pv_tk = {}
                        accfree = {}
                        chain_last = {}
                        yst_tk = {}
                        lastqk_of_ib = {}

                        def load_q(ib):
                            sl = (uoff + ib) % 2
                            cx.wait("sp", lastqk_of_ib.get(ib - 2))
                            a = cx.dinc(sp.dma_start(out=qT[sl][:], in_=QTb[s][:, :, ib * 256:(ib + 1) * 256]), s2["ldq"][sl])
                            cx.wait("sp", chain_last.get((ib - 2, 3)))
                            b_ = cx.dinc(sp.dma_start(
                                out=GG[sl][:],
                                in_=Gs[s][ib * 256:(ib + 1) * 256, 512:1024].rearrange("(q p) f -> p q f", p=128)), s2["ldg"][sl])
                            ld_tk[ib] = (a, b_)

                        def make_gg(ib):
                            sl = (uoff + ib) % 2
                            cx.wait("pool", ld_tk[ib][1])
                            for qs in range(2):
                                tk_ = cx.inc(pool.tensor_tensor(out=GG[sl][:, qs, :], in0=GG[sl][:, qs, :], in1=sgb2, op=ALU.mult),
                                             s2["gg"])
                            gg_tk[ib] = tk_

                        groups = [(ib, hh, jp) for ib in range(NIB) for hh in range(4) for jp in range(16)]
                        NG = len(groups)

                        def emit_qk(g):
                            ib, hh, jp = groups[g]
                            gg_ = goff + g
                            sbi = gg_ % 2
                            sl = (uoff + ib) % 2
                            cx.wait("pe", kv_tk)
                            cx.wait("pe", ld_tk[ib][0])
                            cx.wait("pe", ex_tk.get(g - 2))
                            near = abs(jp - ib) <= 1
                            ins = None
                            for ktl in range(2):
                                kt = 2 * jp + ktl
                                for m in range(2):
                                    o0 = (2 * sbi + m) * 512 + ktl * 256
                                    ins = pe.matmul(ps[:, o0:o0 + 256], lhsT=kT[m * 64:(m + 1) * 64, hh, kt * 128:(kt + 1) * 128],
                                                    rhs=qT[sl][m * 64:(m + 1) * 64, hh, :], start=(ktl == 0), stop=(not near),
                                                    tile_position=(m * 64, 0), skip_group_check=True)
                            if near:
                                for ktl in range(2):
                                    kt = 2 * jp + ktl
                                    uu0 = 256 * ib - 128 * kt + 384
                                    for m in range(2):
                                        o0 = (2 * sbi + m) * 512 + ktl * 256
                                        ins = pe.matmul(ps[:, o0:o0 + 256], lhsT=ident[:], rhs=strip[:, hh, uu0:uu0 + 256],
                                                        start=False, stop=True, skip_group_check=True)
                            qk_tk[g] = cx.inc(ins, s2["qk"])
                            if hh == 3 and jp == 15:
                                lastqk_of_ib[ib] = qk_tk[g]

                        def emit_exp(g):
                            ib, hh, jp = groups[g]
                            gg_ = goff + g
                            sbi = gg_ % 2
                            cx.wait("act", qk_tk[g])
                            if jp < ib - 1:
                                bias = far[:, hh:hh + 1]
                            elif jp > ib + 1:
                                bias = far[:, 4 + hh:5 + hh]
                            else:
                                bias = zero1[:]
                            ex_tk[g] = cx.inc(act.activation(out=PT[gg_ % 3][:], in_=ps[:, sbi * 1024:(sbi + 1) * 1024], func=AF.Exp,
                                                             bias=bias, scale=1.0), s2["ex"])

                        def emit_pv(g):
                            ib, hh, jp = groups[g]
                            gg_ = goff + g
                            u = (uoff + ib) * 4 + hh
                            aset = u % 2
                            cx.wait("pe", ex_tk[g])
                            if jp == 0:
                                cx.wait("pe", accfree.get(u - 2))
                            ins = None
                            for ktl in range(2):
                                kt = 2 * jp + ktl
                                for m in range(2):
                                    for qs in range(2):
                                        o0 = (4 + 2 * aset + m) * 512 + qs * 129
                                        c0 = m * 512 + ktl * 256 + qs * 128
                                        ins = pe.matmul(ps[:, o0:o0 + 129], lhsT=PT[gg_ % 3][:, c0:c0 + 128],
                                                        rhs=vbt[:, kt, hh * 129:(hh + 1) * 129],
                                                        start=(jp == 0 and ktl == 0 and qs == 0),
                                                        stop=(jp == 15 and ktl == 1), skip_group_check=True)
                            pv_tk[g] = cx.inc(ins, s2["pv"])
                            if jp == 15:
                                emit_chain(ib, hh, u, aset, pv_tk[g])

                        def emit_chain(ib, hh, u, aset, acc_tk):
                            sl = (uoff + ib) % 2
                            accs = []
                            for m in range(2):
                                a3 = bank(4 + 2 * aset + m)[:, 0:258].rearrange("p (q e) -> p q e", q=2)
                                accs.append(a3)
                            cx.wait("dve", acc_tk)
                            cx.wait("dve", cx.prev_chain)
                            k0 = cx.inc(dve.reciprocal(out=rr[:, 0:2, :], in_=accs[0][:, :, 128:129]), s2["post"])
                            k1 = cx.inc(dve.reciprocal(out=rr[:, 2:4, :], in_=accs[1][:, :, 128:129]), s2["post"])
                            cx.wait("dve", k1)
                            k2 = cx.inc(dve.tensor_scalar(out=cc[:], in0=rr[:, 2:4, :], scalar1=lamw[:, 4:5], scalar2=None,
                                                          op0=ALU.mult), s2["post"])
                            cx.wait("dve", k2)
                            cx.inc(dve.tensor_tensor(out=t1[:], in0=accs[1][:, :, 0:128], in1=cc[:].broadcast_to([128, 2, 128]),
                                                     op=ALU.mult), s2["post"])
                            k3 = cx.inc(dve.tensor_tensor(out=d0[:], in0=accs[0][:, :, 0:128],
                                                          in1=rr[:, 0:2, :].broadcast_to([128, 2, 128]), op=ALU.mult), s2["post"])
                            accfree[u] = k3
                            cx.wait("dve", k3)
                            k4 = cx.inc(dve.tensor_tensor(out=dd[:], in0=d0[:], in1=t1[:], op=ALU.subtract), s2["post"])
                            cx.wait("dve", k4)
                            k5 = cx.inc(dve.tensor_tensor(out=sq[:], in0=dd[:], in1=dd[:], op=ALU.mult), s2["post"])
                            cx.wait("dve", k5)
                            k6 = cx.inc(dve.reduce_sum(out=ssb[:], in_=sq[:], axis=AX.X), s2["post"])
                            cx.wait("pool", k6)
                            p1_ = cx.inc(pool.tensor_scalar(out=ttb[:], in0=ssb[:], scalar1=1.0 / 128, scalar2=1e-5,
                                                            op0=ALU.mult, op1=ALU.add), s2["postp"])
                            cx.wait("pool", p1_)
                            p2_ = cx.inc(pool.tensor_tensor(out=rsb[:, :, 0], in0=ttb[:], in1=mhalf[:, 0:2], op=ALU.pow), s2["postp"])
                            cx.wait("dve", p2_)
                            k7 = cx.inc(dve.tensor_tensor(out=tmpb[:], in0=dd[:], in1=rsb[:].broadcast_to([128, 2, 128]), op=ALU.mult),
                                        s2["post"])
                            cx.wait("dve", k7)
                            cx.wait("dve", gg_tk[ib])
                            if hh == 0:
                                cx.wait("dve", yst_tk.get(ib - 2))
                            k8 = cx.inc(dve.tensor_tensor(out=ybt[sl][:, :, hh * 128:(hh + 1) * 128], in0=tmpb[:],
                                                          in1=GG[sl][:, :, hh * 128:(hh + 1) * 128], op=ALU.mult), s2["post"])
                            chain_last[(ib, hh)] = k8
                            cx.prev_chain = k8
                            if hh == 3:
                                cx.wait("sp", k8)
                                yst_tk[ib] = cx.dinc(sp.dma_start(
                                    out=Ys[s][ib * 256:(ib + 1) * 256, 512:1024].rearrange("(q p) f -> p q f", p=128),
                                    in_=ybt[sl][:]), s2["yst"][sl])
                                if ib + 2 < NIB:
                                    load_q(ib + 2)
                                if ib + 1 < NIB:
                                    make_gg(ib + 1)

                        load_q(0)
                        load_q(1)
                        make_gg(0)
                        emit_qk(0)
                        emit_qk(1)
                        for g in range(NG):
                            emit_exp(g)
                            if g + 2 < NG:
                                emit_qk(g + 2)
                            emit_pv(g)
                        fin = [(sm, sm.n) for sm in s2["yst"]]
                        for en in ("pe", "act", "dve", "pool", "sp"):
                            cx.wait(en, fin)
                        barrier()
                    if debug == "p2b":
                        continue

                    with ExitStack() as pa:
                        kT = sb("kTa", [128, 4, S], BF16, pa)
                        vat = sb("vat", [128, NT, 520], BF16, pa)
                        qa = [sb(f"qa{i}", [128, 4, 128], BF16, pa) for i in range(4)]
                        ga = [sb(f"ga{i}", [128, 512], F32, pa) for i in range(4)]
                        PT = [sb(f"PTa{i}", [128, 1024], BF16, pa) for i in range(3)]
                        yat = [sb(f"yat{i}", [128, 512], BF16, pa) for i in range(2)]
                        rrA = sb("rrA", [128, 8, 1], F32, pa)
                        tmpA = sb("tmpA", [128, 8, 64], F32, pa)
                        for hh in range(4):
                            cx.dinc(sp.dma_start(out=kT[:, hh, :], in_=KTa[s][:, hh, :]), s2["kv"])
                        for q4 in range(4):
                            cx.dinc(sp.dma_start(
                                out=vat[:, q4 * 8:(q4 + 1) * 8, :],
                                in_=Va[s][q4 * 1024:(q4 + 1) * 1024, :].rearrange("(kt p) e -> p kt e", p=128)), s2["kv"])
                        kv_tk = (s2["kv"], s2["kv"].n)
                        toff = (l * nseq + s) * NT
                        groups = [(t, kt) for t in range(NT) for kt in _na_kts(t)]
                        NG = len(groups)
                        goff = (l * nseq + s) * NG
                        ld_tk = {}
                        qk_tk = {}
                        ex_tk = {}
                        pv_tk = {}
                        accfree = {}
                        yst_tk = {}
                        lastqk_of_t = {}
                        chain_last = {}

                        def load_qa(t):
                            sl4 = (toff + t) % 4
                            cx.wait("sp", lastqk_of_t.get(t - 4))
                            a = cx.dinc(sp.dma_start(out=qa[sl4][:], in_=QTa[s][:, :, t * 128:(t + 1) * 128]), s2["ldqa"][sl4])
                            cx.wait("sp", chain_last.get(t - 4))
                            b_ = cx.dinc(sp.dma_start(out=ga[sl4][:], in_=Gs[s][t * 128:(t + 1) * 128, 0:512]), s2["ldga"][sl4])
                            ld_tk[t] = (a, b_)

                        def emit_qk_a(g):
                            t, kt = groups[g]
                            gg_ = goff + g
                            sbi = gg_ % 2
                            sl = (toff + t) % 4
                            cx.wait("pe", kv_tk)
                            cx.wait("pe", ld_tk[t][0])
                            cx.wait("pe", ex_tk.get(g - 2))
                            for j in range(4):
                                for par in range(2):
                                    o0 = (2 * sbi + par) * 512 + j * 128
                                    pe.matmul(ps[:, o0:o0 + 128], lhsT=kT[par * 64:(par + 1) * 64, j, kt * 128:(kt + 1) * 128],
                                              rhs=qa[sl][par * 64:(par + 1) * 64, j, :], start=(j == 0), stop=False,
                                              tile_position=(par * 64, 0), skip_group_check=True)
                            ci = _COMBO[(t, kt)]
                            pe.matmul(bank(2 * sbi), lhsT=ident[:], rhs=Cb[:, ci, 0:512], start=False, stop=True, skip_group_check=True)
                            ins = pe.matmul(bank(2 * sbi + 1), lhsT=ident[:], rhs=Cb[:, ci, 512:1024], start=False, stop=True,
                                            skip_group_check=True)
                            qk_tk[g] = cx.inc(ins, s2["aqk"])
                            if kt == _na_kts(t)[-1]:
                                lastqk_of_t[t] = qk_tk[g]

                        def emit_exp_a(g):
                            gg_ = goff + g
                            sbi = gg_ % 2
                            cx.wait("act", qk_tk[g])
                            ex_tk[g] = cx.inc(act.activation(out=PT[gg_ % 3][:], in_=ps[:, sbi * 1024:(sbi + 1) * 1024], func=AF.Exp,
                                                             bias=zero1[:], scale=1.0), s2["aex"])

                        def emit_pv_a(g):
                            t, kt = groups[g]
                            gg_ = goff + g
                            u = toff + t
                            aset = u % 2
                            kts = _na_kts(t)
                            cx.wait("pe", ex_tk[g])
                            if kt == kts[0]:
                                cx.wait("pe", accfree.get(t - 2))
                            ins = None
                            for hh in range(8):
                                o0 = (4 + 2 * aset + hh // 4) * 512 + (hh % 4) * 65
                                c0 = (hh % 2) * 512 + (hh // 2) * 128
                                ins = pe.matmul(ps[:, o0:o0 + 65], lhsT=PT[gg_ % 3][:, c0:c0 + 128], rhs=vat[:, kt, hh * 65:(hh + 1) * 65],
                                                start=(kt == kts[0] and hh % 4 == 0), stop=(kt == kts[-1]), skip_group_check=True)
                            pv_tk[g] = cx.inc(ins, s2["apv"])
                            if kt == kts[-1]:
                                emit_chain_a(t, aset, pv_tk[g])

                        def emit_chain_a(t, aset, acc_tk):
                            sl = (toff + t) % 2
                            sl4 = (toff + t) % 4
                            cx.wait("dve", acc_tk)
                            cx.wait("dve", cx.prev_chain_a)
                            accv = [bank(4 + 2 * aset + bk)[:, 0:260].rearrange("p (h e) -> p h e", h=4) for bk in range(2)]
                            cx.inc(dve.reciprocal(out=rrA[:, 0:4, :], in_=accv[0][:, :, 64:65]), s2["apost"])
                            k1 = cx.inc(dve.reciprocal(out=rrA[:, 4:8, :], in_=accv[1][:, :, 64:65]), s2["apost"])
                            cx.wait("dve", k1)
                            cx.inc(dve.tensor_tensor(out=tmpA[:, 0:4, :], in0=accv[0][:, :, 0:64],
                                                     in1=rrA[:, 0:4, :].broadcast_to([128, 4, 64]), op=ALU.mult), s2["apost"])
                            k2 = cx.inc(dve.tensor_tensor(out=tmpA[:, 4:8, :], in0=accv[1][:, :, 0:64],
                                                          in1=rrA[:, 4:8, :].broadcast_to([128, 4, 64]), op=ALU.mult), s2["apost"])
                            accfree[t] = k2
                            cx.wait("dve", k2)
                            cx.wait("dve", ld_tk[t][1])
                            cx.wait("dve", yst_tk.get(t - 2))
                            k3 = cx.inc(dve.tensor_tensor(out=yat[sl][:], in0=tmpA[:].rearrange("p h e -> p (h e)"), in1=ga[sl4][:],
                                                          op=ALU.mult), s2["apost"])
                            chain_last[t] = k3
                            cx.prev_chain_a = k3
                            cx.wait("sp", k3)
                            yst_tk[t] = cx.dinc(sp.dma_start(out=Ys[s][t * 128:(t + 1) * 128, 0:512], in_=yat[sl][:]), s2["ysta"][sl])
                            if t + 3 < NT:
                                load_qa(t + 3)

                        load_qa(0)
                        load_qa(1)
                        load_qa(2)
                        emit_qk_a(0)
                        emit_qk_a(1)
                        for g in range(NG):
                            emit_exp_a(g)
                            if g + 2 < NG:
                                emit_qk_a(g + 2)
                            emit_pv_a(g)
                        fin = [(sm, sm.n) for sm in s2["ysta"]]
                        for en in ("pe", "act", "dve", "pool", "sp"):
                            cx.wait(en, fin)
                        barrier()
                    if debug == "p2a":
                        continue

                    with ExitStack() as p3:
                        yt = [sb(f"yt{i}", [128, D], BF16, p3) for i in range(2)]
                        x3 = [sb(f"x3{i}", [128, D], F32, p3) for i in range(2)]
                        yT = [sb(f"yT{i}", [128, 8, 128], BF16, p3) for i in range(2)]
                        xo = [sb(f"xo{i}", [128, D], F32, p3) for i in range(2)]
                        of = [sb(f"of{i}", [128, D], F32, p3) for i in range(2)]
                        ss3 = sb("ss3", [128, NT], F32, p3)
                        tt3 = sb("tt3", [128, NT], F32, p3)
                        rs3 = sb("rs3", [128, NT], F32, p3)
                        toff = (l * nseq + s) * NT
                        ev_tk = {}
                        mm_tk = {}
                        add_tk = {}
                        st_tk = {}
                        tr_tk = {}
                        dst = out if (last_layer and final) else xs
                        def pre3(t):
                            u = toff + t
                            sl = u % 2
                            tok0 = s * S + t * 128
                            cx.wait("sp", tr_tk.get(t - 2))
                            ldy = cx.dinc(sp.dma_start(out=yt[sl][:], in_=Ys[s][t * 128:(t + 1) * 128, :]), s2["ldy"][sl])
                            cx.wait("sp", add_tk.get(t - 2))
                            ldx_tk[t] = cx.dinc(sp.dma_start(out=x3[sl][:], in_=xsrc[tok0:tok0 + 128, :]), s2["ldx3"][sl])
                            cx.wait("pe", ldy)
                            cx.wait("pe", ev_tk.get(t - 2))
                            trb = bank(sl).bitcast(BF16)
                            for c in range(8):
                                ins = pe.transpose(out=trb[:, c * 128:(c + 1) * 128], in_=yt[sl][:, c * 128:(c + 1) * 128],
                                                   identity=ident[:])
                            tr_tk[t] = cx.inc(ins, s2["tr3"])
                            cx.wait("act", tr_tk[t])
                            cx.wait("act", mm_tk.get(t - 2))
                            ev_tk[t] = cx.inc(act.activation(out=yT[sl][:].rearrange("p c t -> p (c t)"), in_=trb, func=AF.Copy),
                                              s2["ev3"])

                        def main3(t):
                            u = toff + t
                            sl = u % 2
                            tok0 = s * S + t * 128
                            cx.wait("pe", ev_tk[t])
                            cx.wait("pe", add_tk.get(t - 2))
                            for half in range(2):
                                for c in range(8):
                                    ins = pe.matmul(bank(2 + 2 * sl + half), lhsT=yT[sl][:, c, :], rhs=Wo[:, c, half * 512:(half + 1) * 512],
                                                    start=(c == 0), stop=(c == 7))
                            mm_tk[t] = cx.inc(ins, s2["mm3"])
                            cx.wait("dve", mm_tk[t])
                            cx.wait("dve", ldx_tk[t])
                            cx.wait("dve", st_tk.get(t - 2))
                            add_tk[t] = cx.inc(dve.tensor_tensor(out=xo[sl][:], in0=bank(2 + 2 * sl, 2), in1=x3[sl][:], op=ALU.add),
                                               s2["add3"])
                            if last_layer and final:
                                cx.wait("act", add_tk[t])
                                cx.wait("act", cx.junk_tk)
                                q1 = cx.inc(act.activation(out=junk[:], in_=xo[sl][:], func=AF.Square, accum_out=ss3[:, t:t + 1]),
                                            s2["sq3"])
                                cx.junk_tk = q1
                                cx.wait("pool", q1)
                                q2 = cx.inc(pool.tensor_scalar(out=tt3[:, t:t + 1], in0=ss3[:, t:t + 1], scalar1=1.0 / D, scalar2=1e-6,
                                                               op0=ALU.mult, op1=ALU.add), s2["rs3"])
                                cx.wait("pool", q2)
                                q3 = cx.inc(pool.tensor_tensor(out=rs3[:, t:t + 1], in0=tt3[:, t:t + 1], in1=mhalf[:, 0:1], op=ALU.pow),
                                            s2["rs3"])
                                cx.wait("dve", q3)
                                q4 = cx.inc(dve.scalar_tensor_tensor(out=of[sl][:], in0=xo[sl][:], scalar=rs3[:, t:t + 1], in1=fg_rep[:],
                                                                     op0=ALU.mult, op1=ALU.mult), s2["of3"])
                                cx.wait("pool", q4)
                                st_tk[t] = cx.dinc(pool.dma_start(out=dst[tok0:tok0 + 128, :], in_=of[sl][:]), s2["st3"][sl])
                            else:
                                cx.wait("pool", add_tk[t])
                                st_tk[t] = cx.dinc(pool.dma_start(out=dst[tok0:tok0 + 128, :], in_=xo[sl][:]), s2["st3"][sl])

                        ldx_tk = {}
                        pre3(0)
                        for t in range(NT):
                            if t + 1 < NT:
                                pre3(t + 1)
                            main3(t)
                        fin = [(sm, sm.n) for sm in s2["st3"]]
                        for en in ("pe", "act", "dve", "pool", "sp"):
                            cx.wait(en, fin)
                        barrier()
            xsrc = xs

        barrier()
    return nc


def _prep_shared(inputs):
    f32 = np.float32
    t5 = np.asarray(inputs["t5_table"], f32)
    bidx = _t5_strip_idx()
    strip = np.ascontiguousarray(t5[bidx].transpose(0, 2, 1))
    far = np.concatenate([t5[15], t5[31]])[None, :].astype(f32)
    rpb = np.asarray(inputs["na_rpb"], f32)
    nl = rpb.shape[0]
    rpb_ext = np.concatenate([rpb.reshape(nl, -1), np.zeros((nl, 1), f32)], axis=1)
    na_R = np.ascontiguousarray(rpb_ext[:, _RIDX].transpose(0, 2, 1, 3))
    na_mask = np.ascontiguousarray(np.stack([m for (_, m) in _MASKS], axis=1))
    lamv = np.concatenate([np.asarray(inputs[k], f32) for k in ("lambda_q1", "lambda_k1", "lambda_q2", "lambda_k2")], axis=1)
    return {
        "w_in": np.ascontiguousarray(inputs["w_in"], f32),
        "w_out": np.ascontiguousarray(inputs["w_out"], f32),
        "norm_g": np.ascontiguousarray(inputs["norm_g"], f32),
        "final_g": np.ascontiguousarray(np.asarray(inputs["final_g"], f32)[None, :]),
        "subln_g": np.ascontiguousarray(inputs["subln_g"], f32),
        "lamv": np.ascontiguousarray(lamv),
        "t5_strip": strip.astype(f32),
        "t5_far": far,
        "na_R": na_R.astype(f32),
        "na_mask": na_mask.astype(f32),
        "ident": np.eye(128, dtype=f32),
    }


def kernel(**inputs):
    x = np.asarray(inputs["x"], np.float32)
    B = x.shape[0]
    per = B // NCORES
    shared = _prep_shared(inputs)
    nc = build_nc(nseq=per, nlayers=DEPTH)
    in_maps = []
    for c in range(NCORES):
        m = dict(shared)
        m["x"] = np.ascontiguousarray(x[c * per:(c + 1) * per].reshape(per * S, D))
        in_maps.append(m)
    res = run_bass_kernel_spmd(nc, in_maps, core_ids=list(range(NCORES)))
    outs = [np.asarray(r["out"], np.float32).reshape(per, S, D) for r in res.results]
    return np.concatenate(outs, axis=0)
```

```python
import math
from contextlib import ExitStack

import numpy as np
import concourse.bass as bass
import concourse.mybir as mybir
from concourse.bass_utils import run_bass_kernel_spmd

F32 = mybir.dt.float32
BF16 = mybir.dt.bfloat16
AF = mybir.ActivationFunctionType
ALU = mybir.AluOpType
AX = mybir.AxisListType

S = 4096
D = 1024
NT = S // 128
GW = 64
DEPTH = 4
NCORES = 8
NEG = -30000.0


def _t5_bucket(rel):
    n = 16
    ret = np.where(rel > 0, n, 0)
    a = np.abs(rel)
    small = a < 8
    af = np.maximum(a, 1).astype(np.float32)
    v = (np.log(af / np.float32(8)) / np.float32(math.log(16.0)) * np.float32(8)).astype(np.float32)
    large = np.minimum(8 + v.astype(np.int32), n - 1)
    return ret + np.where(small, a, large)


def _t5_strip_idx():
    kl = np.arange(128)[:, None]
    uu = np.arange(896)[None, :]
    return _t5_bucket(kl - (uu - 384))


def _na_kts(t):
    r0, r1 = 2 * t, 2 * t + 1
    rs0 = min(max(r0 - 4, 0), 56)
    rs1 = min(max(r1 - 4, 0), 56)
    lo, hi = min(rs0, rs1), max(rs0, rs1) + 7
    return list(range(lo // 2, hi // 2 + 1))


def _na_tables():
    kl = np.arange(128)
    krl, kc = kl // 64, kl % 64
    ql = np.arange(128)
    qrl, qc = ql // 64, ql % 64
    cs = np.clip(qc - 8, 0, GW - 16)
    colok = (kc[:, None] >= cs[None, :]) & (kc[:, None] < cs[None, :] + 16)
    dc = kc[:, None] - qc[None, :]
    ridx = np.full((7, 128, 8, 128), -1, np.int64)
    for oi in range(7):
        o = oi - 3
        dr = 2 * o + krl[:, None] - qrl[None, :]
        ok = (dr + 7 >= 0) & (dr + 7 <= 14) & (dc + 15 >= 0) & (dc + 15 <= 30)
        base = (dr + 7) * 31 + (dc + 15)
        for h in range(8):
            ridx[oi, :, h, :] = np.where(ok, h * 15 * 31 + base, -1)
    ridx = ridx.reshape(7, 128, 4, 2, 128).transpose(0, 1, 3, 2, 4).reshape(7, 128, 1024)
    masks = []
    combo_of = {}
    key2id = {}
    for t in range(NT):
        for kt in _na_kts(t):
            qr = 2 * t + qrl
            rs = np.clip(qr - 4, 0, 56)
            kr = 2 * kt + krl
            rowok = (kr[:, None] >= rs[None, :]) & (kr[:, None] < rs[None, :] + 8)
            m = np.where(rowok & colok, 0.0, NEG).astype(np.float32)
            key = (kt - t, m.tobytes())
            if key not in key2id:
                key2id[key] = len(masks)
                masks.append((kt - t + 3, m))
            combo_of[(t, kt)] = key2id[key]
    return ridx, masks, combo_of


_RIDX, _MASKS, _COMBO = _na_tables()
NCOMBO = len(_MASKS)


class Sem:
    def __init__(self, nc, stack, name):
        self.h = stack.enter_context(nc.semaphore(name))
        self.n = 0
        self.name = name


class Ctx:
    def __init__(self, nc, stack):
        self.nc = nc
        self.stack = stack
        self.waited = {}
        self.junk_tk = None
        self.prev_chain = None
        self.prev_chain_a = None
        self.nsem = 0
        self.eng = {"pe": nc.tensor, "act": nc.scalar, "dve": nc.vector, "pool": nc.gpsimd, "sp": nc.sync}

    def sem(self, name):
        self.nsem += 1
        return Sem(self.nc, self.stack, name)

    def inc(self, ins, sem, amt=1):
        ins.then_inc(sem.h, amt)
        sem.n += amt
        return (sem, sem.n)

    def dinc(self, ins, sem):
        return self.inc(ins, sem, 16)

    def wait(self, en, tk):
        if tk is None:
            return
        if isinstance(tk, list):
            for t in tk:
                self.wait(en, t)
            return
        sem, val = tk
        key = (en, sem.name)
        if self.waited.get(key, 0) >= val:
            return
        self.waited[key] = val
        self.eng[en].wait_ge(sem.h, val)


def build_nc(nseq=2, nlayers=DEPTH, final=True, debug=False):
    nc = bass.Bass("TRN2", target_bir_lowering=False)
    NTOK = nseq * S
    x_in = nc.dram_tensor("x", [NTOK, D], F32, kind="ExternalInput").ap()
    w_in = nc.dram_tensor("w_in", [nlayers, D, 4096], F32, kind="ExternalInput").ap()
    w_out = nc.dram_tensor("w_out", [nlayers, D, D], F32, kind="ExternalInput").ap()
    norm_g = nc.dram_tensor("norm_g", [nlayers, D], F32, kind="ExternalInput").ap()
    final_g = nc.dram_tensor("final_g", [1, D], F32, kind="ExternalInput").ap()
    subln_g = nc.dram_tensor("subln_g", [nlayers, 128], F32, kind="ExternalInput").ap()
    lamv = nc.dram_tensor("lamv", [nlayers, 4 * 64], F32, kind="ExternalInput").ap()
    t5_strip = nc.dram_tensor("t5_strip", [128, 4, 896], F32, kind="ExternalInput").ap()
    t5_far = nc.dram_tensor("t5_far", [1, 8], F32, kind="ExternalInput").ap()
    na_R = nc.dram_tensor("na_R", [nlayers, 128, 7, 1024], F32, kind="ExternalInput").ap()
    na_mask = nc.dram_tensor("na_mask", [128, NCOMBO, 128], F32, kind="ExternalInput").ap()
    ident_in = nc.dram_tensor("ident", [128, 128], F32, kind="ExternalInput").ap()
    out = nc.dram_tensor("out", [NTOK, D], F32, kind="ExternalOutput").ap()
    okind = "ExternalOutput" if debug else "Internal"
    xs = nc.dram_tensor("xs", [NTOK, D], F32, kind="Internal").ap()
    QTa = [nc.dram_tensor(f"QTa{s}", [128, 4, S], BF16, kind=okind).ap() for s in range(nseq)]
    KTa = [nc.dram_tensor(f"KTa{s}", [128, 4, S], BF16, kind=okind).ap() for s in range(nseq)]
    QTb = [nc.dram_tensor(f"QTb{s}", [128, 4, S], BF16, kind=okind).ap() for s in range(nseq)]
    KTb = [nc.dram_tensor(f"KTb{s}", [128, 4, S], BF16, kind=okind).ap() for s in range(nseq)]
    Va = [nc.dram_tensor(f"Va{s}", [S, 520], BF16, kind=okind).ap() for s in range(nseq)]
    Vb = [nc.dram_tensor(f"Vb{s}", [S, 516], BF16, kind=okind).ap() for s in range(nseq)]
    Gs = [nc.dram_tensor(f"G{s}", [S, D], F32, kind=okind).ap() for s in range(nseq)]
    Ys = [nc.dram_tensor(f"Y{s}", [S, D], BF16, kind=okind).ap() for s in range(nseq)]

    with ExitStack() as stack:
        cx = Ctx(nc, stack)
        E = cx.eng
        pe, act, dve, pool, sp = E["pe"], E["act"], E["dve"], E["pool"], E["sp"]

        uniq = [0]

        def sb(name, shape, dt, st=stack):
            uniq[0] += 1
            return st.enter_context(nc.sbuf_tensor(f"sb{uniq[0]}_{name}", shape, dt))

        ps = stack.enter_context(nc.psum_tensor("ps", [128, 4096], F32))
        stack.enter_context(nc.Block())

        def bank(b, n=1):
            return ps[:, b * 512:(b + n) * 512]

        ident = sb("ident", [128, 128], BF16)
        strip = sb("strip", [128, 4, 896], BF16)
        far = sb("far", [128, 8], F32)
        zero1 = sb("zero1", [128, 1], F32)
        maskb = sb("maskb", [128, NCOMBO, 128], BF16)
        fg_rep = sb("fg_rep", [128, D], F32)
        g_rep = sb("g_rep", [128, D], F32)
        sgb = sb("sgb", [128, 4, 128], F32)
        lamt = sb("lamt", [128, 4 * 64], F32)
        lamw = sb("lamw", [128, 8], F32)
        junk = sb("junk", [128, D], BF16)
        mhalf = sb("mhalf", [128, NT], F32)

        s_setup = cx.sem("setup")
        s_bar = cx.sem("bar")

        def barrier():
            for en in ("pe", "act", "dve", "pool", "sp"):
                cx.inc(E[en].drain(), s_bar)
            tk = (s_bar, s_bar.n)
            for en in ("pe", "act", "dve", "pool", "sp"):
                cx.wait(en, tk)

        s_setup_sw = cx.sem("setupsw")
        cx.dinc(pool.dma_start(out=ident[:], in_=ident_in), s_setup_sw)
        cx.dinc(pool.dma_start(out=strip[:], in_=t5_strip), s_setup_sw)
        cx.dinc(pool.dma_start(out=maskb[:], in_=na_mask), s_setup_sw)
        cx.dinc(sp.dma_start(out=far[:], in_=t5_far.partition_broadcast(128)), s_setup)
        cx.dinc(sp.dma_start(out=fg_rep[:], in_=final_g.partition_broadcast(128)), s_setup)
        setup_tk = [(s_setup, s_setup.n), (s_setup_sw, s_setup_sw.n)]
        s_ms = cx.sem("ms")
        dve.memset(mhalf[:], -0.5)
        ms_tk = cx.inc(dve.memset(zero1[:], 0.0), s_ms)
        for en in ("pe", "act", "dve", "pool", "sp"):
            cx.wait(en, setup_tk)
            cx.wait(en, ms_tk)

        s_lay = cx.sem("lay")
        s_layc = cx.sem("layc")
        s_w1 = cx.sem("w1")
        cx.s_w1g = [cx.sem(f"w1g{i}") for i in range(8)]
        s_wo = cx.sem("wo")

        xsrc = x_in
        for l in range(nlayers):
            lam_init = 0.8 - 0.6 * math.exp(-0.3 * l)
            last_layer = (l == nlayers - 1)
            cx.dinc(sp.dma_start(out=g_rep[:], in_=norm_g[l:l + 1, :].partition_broadcast(128)), s_lay)
            cx.dinc(sp.dma_start(out=lamt[:], in_=lamv[l:l + 1, :].partition_broadcast(128)), s_lay)
            tk = cx.dinc(sp.dma_start(out=sgb[:, 0, :], in_=subln_g[l:l + 1, :].partition_broadcast(128)), s_lay)
            cx.wait("dve", tk)
            cx.wait("dve", cx.junk_tk)
            t1 = cx.inc(dve.tensor_tensor(out=junk[:, 0:64], in0=lamt[:, 0:64], in1=lamt[:, 64:128], op=ALU.mult), s_layc)
            t2 = cx.inc(dve.tensor_tensor(out=junk[:, 64:128], in0=lamt[:, 128:192], in1=lamt[:, 192:256], op=ALU.mult), s_layc)
            cx.wait("dve", t2)
            t2b = cx.inc(dve.reduce_sum(out=lamw[:, 0:1], in_=junk[:, 0:64], axis=AX.X), s_layc)
            cx.wait("dve", t2b)
            t3 = cx.inc(dve.reduce_sum(out=lamw[:, 1:2], in_=junk[:, 64:128], axis=AX.X), s_layc)
            cx.junk_tk = t3
            cx.wait("act", t3)
            t4 = cx.inc(act.activation(out=lamw[:, 2:4], in_=lamw[:, 0:2], func=AF.Exp), s_layc)
            cx.wait("dve", t4)
            t5 = cx.inc(dve.tensor_tensor(out=lamw[:, 5:6], in0=lamw[:, 2:3], in1=lamw[:, 3:4], op=ALU.subtract), s_layc)
            cx.wait("dve", t5)
            t6 = cx.inc(dve.tensor_scalar(out=lamw[:, 4:5], in0=lamw[:, 5:6], scalar1=float(lam_init), scalar2=None, op0=ALU.add), s_layc)
            t7 = cx.inc(dve.tensor_scalar(out=sgb[:, 0, :], in0=sgb[:, 0, :], scalar1=float(1.0 - lam_init), scalar2=None, op0=ALU.mult), s_layc)
            cx.wait("dve", t7)
            for hh in range(1, 4):
                t8 = cx.inc(dve.tensor_copy(out=sgb[:, hh, :], in_=sgb[:, 0, :]), s_layc)
            lay_tk = [t6, t8, (s_lay, s_lay.n)]
            for en in ("pe", "act", "dve", "pool", "sp"):
                cx.wait(en, lay_tk)

            lay = ExitStack()
            Wo = sb("Wo", [128, 8, D], BF16, lay)
            Cb = sb("Cb", [128, NCOMBO, 1024], BF16, lay)
            Rb = sb("Rb", [128, 7, 1024], BF16, lay)
            if l == 0:
                cx.s_cb = cx.sem("laycb")
            with ExitStack() as p1:
                W1 = sb("W1", [128, 8, 4096], BF16, p1)
                xt = [sb(f"xt{i}", [128, D], F32, p1) for i in range(4)]
                hb = [sb(f"hb{i}", [128, D], BF16, p1) for i in range(4)]
                hT = [sb(f"hT{i}", [128, 8, 512], BF16, p1) for i in range(2)]
                stgF = [sb(f"stgF{i}", [128, 4, 512], BF16, p1) for i in range(3)]
                stgVa = [sb(f"stgVa{i}", [128, 8, 65], BF16, p1) for i in range(2)]
                stgVb = [sb(f"stgVb{i}", [128, 4, 129], BF16, p1) for i in range(2)]
                stgG = [sb(f"stgG{i}", [128, D], F32, p1) for i in range(2)]
                ssq = sb("ssq", [128, NT], F32, p1)
                rtmp = sb("rtmp", [128, NT], F32, p1)
                rstd = sb("rstd", [128, NT], F32, p1)
                if l == 0:
                    s_p1 = {k: cx.sem("p1" + k) for k in
                            ("ss", "rs", "h", "tr", "hte", "fmm", "tmm", "fevdve", "fevact", "tevdve", "tevact", "ones")}
                    s_p1["ldx"] = [cx.sem(f"p1ldx{i}") for i in range(4)]
                    s_p1["stF"] = [cx.sem(f"p1stF{i}") for i in range(3)]
                    s_p1["stV"] = [cx.sem(f"p1stV{i}") for i in range(2)]
                    cx.s_p1 = s_p1
                s_p1 = cx.s_p1
                for i in range(2):
                    pool.memset(stgVa[i][:, :, 64:65], 1.0)
                    ones_tk = cx.inc(pool.memset(stgVb[i][:, :, 128:129], 1.0), s_p1["ones"])
                cx.wait("dve", ones_tk)
                cx.wait("act", ones_tk)
                cx.wait("sp", ones_tk)

                rel_xt = {}
                rel_hb = {}
                rel_tr = {}
                rel_hT = {}
                rel_psF = {}
                rel_psT = {}
                rel_stF = {}
                rel_stV = {}
                cnt = {"nF": 0, "nT": 0, "nSF": 0, "nSV": 0}
                hte_last = {}
                h_tk = {}
                a1_tk = {}
                NX = len(xt)
                NH = len(hb)

                ldx_of = {}

                def stageL(s, b):
                    ldxs = {}
                    for ii in range(4):
                        i = b * 4 + ii
                        u = s * NT + i
                        tok0 = s * S + i * 128
                        cx.wait("pool", rel_xt.get(u - NX))
                        ldxs[ii] = cx.dinc(pool.dma_start(out=xt[u % NX][:], in_=xsrc[tok0:tok0 + 128, :]), s_p1["ldx"][u % NX])
                    ldx_of[(s, b)] = ldxs

                def stageA(s, b):
                    if (s, b) not in ldx_of:
                        stageL(s, b)
                    ldxs = ldx_of[(s, b)]
                    ssts = {}
                    for ii in range(4):
                        i = b * 4 + ii
                        u = s * NT + i
                        cx.wait("act", ldxs[ii])
                        cx.wait("act", cx.junk_tk)
                        ssts[ii] = cx.inc(act.activation(out=junk[:], in_=xt[u % NX][:], func=AF.Square,
                                                         accum_out=ssq[:, i:i + 1]), s_p1["ss"])
                        cx.junk_tk = ssts[ii]
                    i0 = b * 4
                    cx.wait("pool", ssts[3])
                    r1 = cx.inc(pool.tensor_scalar(out=rtmp[:, i0:i0 + 4], in0=ssq[:, i0:i0 + 4], scalar1=1.0 / D,
                                                   scalar2=1e-6, op0=ALU.mult, op1=ALU.add), s_p1["rs"])
                    cx.wait("pool", r1)
                    r2 = cx.inc(pool.tensor_tensor(out=rstd[:, i0:i0 + 4], in0=rtmp[:, i0:i0 + 4], in1=mhalf[:, 0:4],
                                                   op=ALU.pow), s_p1["rs"])
                    a1_tk[(s, b)] = (r2, ldxs)

                def stageA2(s, b):
                    r2, ldxs = a1_tk[(s, b)]
                    for ii in range(4):
                        i = b * 4 + ii
                        u = s * NT + i
                        cx.wait("dve", r2)
                        cx.wait("dve", rel_hb.get(u - NH))
                        cx.wait("dve", ldxs[ii])
                        htk = cx.inc(dve.scalar_tensor_tensor(out=hb[u % NH][:], in0=xt[u % NX][:], scalar=rstd[:, i:i + 1],
                                                              in1=g_rep[:], op0=ALU.mult, op1=ALU.mult), s_p1["h"])
                        rel_xt[u] = htk
                        h_tk[u] = htk

                def stageB(s, b):
                    for ii in range(4):
                        i = b * 4 + ii
                        u = s * NT + i
                        cx.wait("pe", h_tk[u])
                        cx.wait("pe", rel_tr.get(u - 2))
                        trb = bank(u % 2).bitcast(BF16)
                        for c in range(8):
                            ins = pe.transpose(out=trb[:, c * 128:(c + 1) * 128], in_=hb[u % NH][:, c * 128:(c + 1) * 128],
                                               identity=ident[:])
                        trk = cx.inc(ins, s_p1["tr"])
                        rel_hb[u] = trk
                        cx.wait("act", trk)
                        if ii == 0:
                            cx.wait("act", rel_hT.get((s * 8 + b) - 2))
                        hte = cx.inc(act.activation(out=hT[b % 2][:, :, ii * 128:(ii + 1) * 128],
                                                    in_=trb.rearrange("p (c t) -> p c t", c=8), func=AF.Copy), s_p1["hte"])
                        rel_tr[u] = hte
                        hte_last[(s, b)] = hte

                def main1(s, b, mid=None, mid_f=None):
                    cx.wait("pe", hte_last[(s, b)])
                    hTb = hT[b % 2]
                    col_bases = [0, 512, 2048, 2560]
                    dsts = [QTa[s], KTa[s], QTb[s], KTb[s]]
                    for gi in range(4):
                        nSF = cnt["nSF"]
                        sf = nSF % 3
                        cx.wait("pe", w1_tk[col_bases[gi]])
                        for cb in range(4):
                            nF = cnt["nF"]
                            col0 = col_bases[gi] + cb * 128
                            pb = 2 + (nF % 3)
                            cx.wait("pe", rel_psF.get(nF - 3))
                            for c in range(8):
                                ins = pe.matmul(bank(pb), lhsT=W1[:, c, col0:col0 + 128], rhs=hTb[:, c, :],
                                                start=(c == 0), stop=(c == 7))
                            fmm = cx.inc(ins, s_p1["fmm"])
                            scale = 0.125 if gi in (0, 2) else 1.0
                            en = "dve" if (nF % 2 == 0) else "act"
                            cx.wait(en, fmm)
                            cx.wait(en, rel_stF.get(nSF - 3))
                            if en == "dve":
                                ins = dve.tensor_scalar(out=stgF[sf][:, cb, :], in0=bank(pb), scalar1=scale, scalar2=None,
                                                        op0=ALU.mult)
                            else:
                                ins = act.activation(out=stgF[sf][:, cb, :], in_=bank(pb), func=AF.Copy, scale=scale)
                            fev = cx.inc(ins, s_p1["fev" + en])
                            rel_psF[nF] = fev
                            cnt["nF"] += 1
                        cx.wait("sp", (s_p1["fevdve"], s_p1["fevdve"].n))
                        cx.wait("sp", (s_p1["fevact"], s_p1["fevact"].n))
                        stk = cx.dinc(sp.dma_start(out=dsts[gi][:, :, b * 512:(b + 1) * 512], in_=stgF[sf][:]),
                                      s_p1["stF"][sf])
                        rel_stF[nSF] = stk
                        cnt["nSF"] += 1
                        if gi == 1 and mid_f is not None:
                            mid_f()
                    if mid is not None:
                        mid()
                    for ii in range(4):
                        i = b * 4 + ii
                        tok0l = i * 128
                        nSV = cnt["nSV"]
                        sv = nSV % 2
                        cx.wait("dve", rel_stV.get(nSV - 2))
                        cx.wait("act", rel_stV.get(nSV - 2))
                        for gq, col0 in enumerate((1024, 1536, 3072, 3584)):
                            nT = cnt["nT"]
                            pb = 5 + (nT % 3)
                            cx.wait("pe", w1_tk[col0])
                            cx.wait("pe", rel_psT.get(nT - 3))
                            for c in range(8):
                                ins = pe.matmul(bank(pb), lhsT=hTb[:, c, ii * 128:(ii + 1) * 128],
                                                rhs=W1[:, c, col0:col0 + 512], start=(c == 0), stop=(c == 7))
                            tmm = cx.inc(ins, s_p1["tmm"])
                            if gq == 0:
                                cx.wait("dve", tmm)
                                ins = dve.tensor_copy(out=stgVa[sv][:, :, 0:64],
                                                      in_=bank(pb).rearrange("p (h e) -> p h e", h=8))
                            elif gq == 2:
                                cx.wait("dve", tmm)
                                ins = dve.tensor_copy(out=stgVb[sv][:, :, 0:128],
                                                      in_=bank(pb).rearrange("p (h e) -> p h e", h=4))
                            else:
                                cx.wait("act", tmm)
                                o0 = 0 if gq == 1 else 512
                                ins = act.activation(out=stgG[sv][:, o0:o0 + 512], in_=bank(pb), func=AF.Silu)
                            tev = cx.inc(ins, s_p1["tevdve" if gq in (0, 2) else "tevact"])
                            rel_psT[nT] = tev
                            cnt["nT"] += 1
                        if ii == 3:
                            rel_hT[s * 8 + b] = tmm
                        cx.wait("sp", (s_p1["tevdve"], s_p1["tevdve"].n))
                        cx.wait("sp", (s_p1["tevact"], s_p1["tevact"].n))
                        sp.dma_start(out=Va[s][tok0l:tok0l + 128, :], in_=stgVa[sv][:].rearrange("p h e -> p (h e)")
                                     ).then_inc(s_p1["stV"][sv].h, 16)
                        sp.dma_start(out=Vb[s][tok0l:tok0l + 128, :], in_=stgVb[sv][:].rearrange("p h e -> p (h e)")
                                     ).then_inc(s_p1["stV"][sv].h, 16)
                        s_p1["stV"][sv].n += 32
                        stk = cx.dinc(sp.dma_start(out=Gs[s][tok0l:tok0l + 128, :], in_=stgG[sv][:]), s_p1["stV"][sv])
                        rel_stV[nSV] = stk
                        cnt["nSV"] += 1

                blocks = [(s, b) for s in range(nseq) for b in range(S // 512)]
                stageL(*blocks[0])
                for c in range(8):
                    cx.dinc(pool.dma_start(out=W1[:, c, :], in_=w_in[l, c * 128:(c + 1) * 128, :]), s_w1)
                w1_all = (s_w1, s_w1.n)
                for c in range(8):
                    cx.dinc(pool.dma_start(out=Wo[:, c, :], in_=w_out[l, c * 128:(c + 1) * 128, :]), s_wo)
                cx.dinc(pool.dma_start(out=Rb[:], in_=na_R[l]), s_wo)
                wo_tk = (s_wo, s_wo.n)

                def build_cb():
                    cx.wait("pool", wo_tk)
                    for ci, (oi, _m) in enumerate(_MASKS):
                        cx.cb_tk = cx.inc(pool.tensor_tensor(
                            out=Cb[:, ci, :].rearrange("p (h q) -> p h q", h=8),
                            in0=Rb[:, oi, :].rearrange("p (h q) -> p h q", h=8),
                            in1=maskb[:, ci:ci + 1, :].broadcast_to([128, 8, 128]), op=ALU.add), cx.s_cb)
                w1_tk = {col0: w1_all for col0 in (0, 512, 2048, 2560, 1024, 1536, 3072, 3584)}
                stageA(*blocks[0])
                stageA2(*blocks[0])
                stageB(*blocks[0])
                for k, (s, b) in enumerate(blocks):
                    if k == 3:
                        build_cb()
                    if k + 1 < len(blocks):
                        nb = blocks[k + 1]
                        stageA(*nb)
                        main1(s, b, mid=lambda nb=nb: stageB(*nb), mid_f=lambda nb=nb: stageA2(*nb))
                    else:
                        main1(s, b)
                fin = [(sm, sm.n) for sm in s_p1["stF"] + s_p1["stV"]]
                for en in ("pe", "act", "dve", "pool", "sp"):
                    cx.wait(en, fin)
                barrier()

            if debug == "p1":
                break
            if l == 0:
                s2 = {k: cx.sem("s2" + k) for k in
                      ("cb", "kv", "kva", "qk", "ex", "pv", "post", "postp", "gg",
                       "aqk", "aex", "apv", "apost", "tr3", "ev3", "mm3", "add3", "sq3", "rs3", "of3")}
                for k, n in (("ldq", 2), ("ldg", 2), ("yst", 2), ("ldqa", 4), ("ldga", 4), ("ysta", 2),
                             ("ldy", 2), ("ldx3", 2), ("st3", 2)):
                    s2[k] = [cx.sem(f"s2{k}{i}") for i in range(n)]
                cx.s2 = s2
            s2 = cx.s2
            with ExitStack() as p2:
                p2.enter_context(lay)
                for en in ("pe", "act", "dve", "pool", "sp"):
                    cx.wait(en, cx.cb_tk)
                    cx.wait(en, wo_tk)

                for s in range(nseq):
                    seqst = ExitStack()
                    kTa_pre = sb("kTa", [128, 4, S], BF16, seqst)
                    with ExitStack() as pb:
                        kT = sb("kT", [128, 4, S], BF16, pb)
                        vbt = sb("vbt", [128, NT, 516], BF16, pb)
                        qT = [sb(f"qT{i}", [128, 4, 256], BF16, pb) for i in range(2)]
                        GG = [sb(f"GG{i}", [128, 2, 512], F32, pb) for i in range(2)]
                        PT = [sb(f"PT{i}", [128, 1024], BF16, pb) for i in range(3)]
                        ybt = [sb(f"ybt{i}", [128, 2, 512], BF16, pb) for i in range(2)]
                        rr = sb("rr", [128, 4, 1], F32, pb)
                        cc = sb("cc", [128, 2, 1], F32, pb)
                        t1 = sb("t1", [128, 2, 128], F32, pb)
                        d0 = sb("d0", [128, 2, 128], F32, pb)
                        dd = sb("dd", [128, 2, 128], F32, pb)
                        sq = sb("sq", [128, 2, 128], F32, pb)
                        ssb = sb("ssb", [128, 2], F32, pb)
                        ttb = sb("ttb", [128, 2], F32, pb)
                        rsb = sb("rsb", [128, 2, 1], F32, pb)
                        tmpb = sb("tmpb", [128, 2, 128], F32, pb)
                        sgb2 = sgb[:].rearrange("p h e -> p (h e)")
                        for hh in range(4):
                            cx.dinc(sp.dma_start(out=kT[:, hh, :], in_=KTb[s][:, hh, :]), s2["kv"])
                        for q4 in range(4):
                            cx.dinc(sp.dma_start(
                                out=vbt[:, q4 * 8:(q4 + 1) * 8, :],
                                in_=Vb[s][q4 * 1024:(q4 + 1) * 1024, :].rearrange("(kt p) e -> p kt e", p=128)), s2["kv"])
                        kv_tk = (s2["kv"], s2["kv"].n)
                        for hh in range(4):
                            cx.dinc(sp.dma_start(out=kTa_pre[:, hh, :], in_=KTa[s][:, hh, :]), s2["kva"])
                        kva_tk = (s2["kva"], s2["kva"].n)

                        NIB = S // 256
                        uoff = (l * nseq + s) * NIB
                        goff = (l * nseq + s) * (NIB * 4 * 16)
                        ld_tk = {}
                        gg_tk = {}
                        qk_tk = {}
                        ex_tk = {}
                        pv_tk = {}
                        accfree = {}
                        chain_last = {}
                        yst_tk = {}
                        lastqk_of_ib = {}

                        def load_q(ib):
                            sl = (uoff + ib) % 2
                            cx.wait("sp", lastqk_of_ib.get(ib - 2))
                            a = cx.dinc(sp.dma_start(out=qT[sl][:], in_=QTb[s][:, :, ib * 256:(ib + 1) * 256]), s2["ldq"][sl])
                            cx.wait("sp", chain_last.get((ib - 2, 3)))
                            b_ = cx.dinc(sp.dma_start(
                                out=GG[sl][:],
                                in_=Gs[s][ib * 256:(ib + 1) * 256, 512:1024].rearrange("(q p) f -> p q f", p=128)), s2["ldg"][sl])
                            ld_tk[ib] = (a, b_)

                        def make_gg(ib):
                            sl = (uoff + ib) % 2
                            cx.wait("pool", ld_tk[ib][1])
                            for qs in range(2):
                                tk_ = cx.inc(pool.tensor_tensor(out=GG[sl][:, qs, :], in0=GG[sl][:, qs, :], in1=sgb2, op=ALU.mult),
                                             s2["gg"])
                            gg_tk[ib] = tk_

                        groups = [(ib, hh, jp) for ib in range(NIB) for hh in range(4) for jp in range(16)]
                        NG = len(groups)

                        def emit_qk(g):
                            ib, hh, jp = groups[g]
                            gg_ = goff + g
                            sbi = gg_ % 2
                            sl = (uoff + ib) % 2
                            cx.wait("pe", kv_tk)
                            cx.wait("pe", ld_tk[ib][0])
                            cx.wait("pe", ex_tk.get(g - 2))
                            near = abs(jp - ib) <= 1
                            ins = None
                            for ktl in range(2):
                                kt = 2 * jp + ktl
                                for m in range(2):
                                    o0 = (2 * sbi + m) * 512 + ktl * 256
                                    ins = pe.matmul(ps[:, o0:o0 + 256], lhsT=kT[m * 64:(m + 1) * 64, hh, kt * 128:(kt + 1) * 128],
                                                    rhs=qT[sl][m * 64:(m + 1) * 64, hh, :], start=(ktl == 0), stop=(not near),
                                                    tile_position=(m * 64, 0), skip_group_check=True)
                            if near:
                                for ktl in range(2):
                                    kt = 2 * jp + ktl
                                    uu0 = 256 * ib - 128 * kt + 384
                                    for m in range(2):
                                        o0 = (2 * sbi + m) * 512 + ktl * 256
                                        ins = pe.matmul(ps[:, o0:o0 + 256], lhsT=ident[:], rhs=strip[:, hh, uu0:uu0 + 256],
                                                        start=False, stop=True, skip_group_check=True)
                            qk_tk[g] = cx.inc(ins, s2["qk"])
                            if hh == 3 and jp == 15:
                                lastqk_of_ib[ib] = qk_tk[g]

                        def emit_exp(g):
                            ib, hh, jp = groups[g]
                            gg_ = goff + g
                            sbi = gg_ % 2
                            cx.wait("act", qk_tk[g])
                            if jp < ib - 1:
                                bias = far[:, hh:hh + 1]
                            elif jp > ib + 1:
                                bias = far[:, 4 + hh:5 + hh]
                            else:
                                bias = zero1[:]
                            ex_tk[g] = cx.inc(act.activation(out=PT[gg_ % 3][:], in_=ps[:, sbi * 1024:(sbi + 1) * 1024], func=AF.Exp,
                                                             bias=bias, scale=1.0), s2["ex"])

                        def emit_pv(g):
                            ib, hh, jp = groups[g]
                            gg_ = goff + g
                            u = (uoff + ib) * 4 + hh
                            aset = u % 2
                            cx.wait("pe", ex_tk[g])
                            if jp == 0:
                                cx.wait("pe", accfree.get(u - 2))
                            ins = None
                            for ktl in range(2):
                                kt = 2 * jp + ktl
                                for m in range(2):
                                    for qs in range(2):
                                        o0 = (4 + 2 * aset + m) * 512 + qs * 129
                                        c0 = m * 512 + ktl * 256 + qs * 128
                                        ins = pe.matmul(ps[:, o0:o0 + 129], lhsT=PT[gg_ % 3][:, c0:c0 + 128],
                                                        rhs=vbt[:, kt, hh * 129:(hh + 1) * 129],
                                                        start=(jp == 0 and ktl == 0 and qs == 0),
                                                        stop=(jp == 15 and ktl == 1), skip_group_check=True)
                            pv_tk[g] = cx.inc(ins, s2["pv"])
                            if jp == 15:
                                emit_chain(ib, hh, u, aset, pv_tk[g])

                        def emit_chain(ib, hh, u, aset, acc_tk):
                            sl = (uoff + ib) % 2
                            accs = []
                            for m in range(2):
                                a3 = bank(4 + 2 * aset + m)[:, 0:258].rearrange("p (q e) -> p q e", q=2)
                                accs.append(a3)
                            cx.wait("dve", acc_tk)
                            cx.wait("dve", cx.prev_chain)
                            k0 = cx.inc(dve.reciprocal(out=rr[:, 0:2, :], in_=accs[0][:, :, 128:129]), s2["post"])
                            k1 = cx.inc(dve.reciprocal(out=rr[:, 2:4, :], in_=accs[1][:, :, 128:129]), s2["post"])
                            cx.wait("dve", k1)
                            k2 = cx.inc(dve.tensor_scalar(out=cc[:], in0=rr[:, 2:4, :], scalar1=lamw[:, 4:5], scalar2=None,
                                                          op0=ALU.mult), s2["post"])
                            cx.wait("dve", k2)
                            cx.inc(dve.tensor_tensor(out=t1[:], in0=accs[1][:, :, 0:128], in1=cc[:].broadcast_to([128, 2, 128]),
                                                     op=ALU.mult), s2["post"])
                            k3 = cx.inc(dve.tensor_tensor(out=d0[:], in0=accs[0][:, :, 0:128],
                                                          in1=rr[:, 0:2, :].broadcast_to([128, 2, 128]), op=ALU.mult), s2["post"])
                            accfree[u] = k3
                            cx.wait("dve", k3)
                            k4 = cx.inc(dve.tensor_tensor(out=dd[:], in0=d0[:], in1=t1[:], op=ALU.subtract), s2["post"])
                            cx.wait("dve", k4)
                            k5 = cx.inc(dve.tensor_tensor(out=sq[:], in0=dd[:], in1=dd[:], op=ALU.mult), s2["post"])
                            cx.wait("dve", k5)
                            k6 = cx.inc(dve.reduce_sum(out=ssb[:], in_=sq[:], axis=AX.X), s2["post"])
                            cx.wait("pool", k6)
                            p1_ = cx.inc(pool.tensor_scalar(out=ttb[:], in0=ssb[:], scalar1=1.0 / 128, scalar2=1e-5,
                                                            op0=ALU.mult, op1=ALU.add), s2["postp"])
                            cx.wait("pool", p1_)
                            p2_ = cx.inc(pool.tensor_tensor(out=rsb[:, :, 0], in0=ttb[:], in1=mhalf[:, 0:2], op=ALU.pow), s2["postp"])
                            cx.wait("dve", p2_)
                            k7 = cx.inc(dve.tensor_tensor(out=tmpb[:], in0=dd[:], in1=rsb[:].broadcast_to([128, 2, 128]), op=ALU.mult),
                                        s2["post"])
                            cx.wait("dve", k7)
                            cx.wait("dve", gg_tk[ib])
                            if hh == 0:
                                cx.wait("dve", yst_tk.get(ib - 2))
                            k8 = cx.inc(dve.tensor_tensor(out=ybt[sl][:, :, hh * 128:(hh + 1) * 128], in0=tmpb[:],
                                                          in1=GG[sl][:, :, hh * 128:(hh + 1) * 128], op=ALU.mult), s2["post"])
                            chain_last[(ib, hh)] = k8
                            cx.prev_chain = k8
                            if hh == 3:
                                cx.wait("sp", k8)
                                yst_tk[ib] = cx.dinc(sp.dma_start(
                                    out=Ys[s][ib * 256:(ib + 1) * 256, 512:1024].rearrange("(q p) f -> p q f", p=128),
                                    in_=ybt[sl][:]), s2["yst"][sl])
                                if ib + 2 < NIB:
                                    load_q(ib + 2)
                                if ib + 1 < NIB:
                                    make_gg(ib + 1)

                        load_q(0)
                        load_q(1)
                        make_gg(0)
                        emit_qk(0)
                        emit_qk(1)
                        for g in range(NG):
                            emit_exp(g)
                            if g + 2 < NG:
                                emit_qk(g + 2)
                            emit_pv(g)
                        fin = [(sm, sm.n) for sm in s2["yst"]]
                        for en in ("pe", "act", "dve", "pool", "sp"):
                            cx.wait(en, fin)
                        barrier()
                    if debug == "p2b":
                        continue

                    with ExitStack() as pa:
                        kT = kTa_pre
                        vat = sb("vat", [128, NT, 520], BF16, pa)
                        qa = [sb(f"qa{i}", [128, 4, 128], BF16, pa) for i in range(4)]
                        ga = [sb(f"ga{i}", [128, 512], F32, pa) for i in range(4)]
                        PT = [sb(f"PTa{i}", [128, 1024], BF16, pa) for i in range(3)]
                        yat = [sb(f"yat{i}", [128, 512], BF16, pa) for i in range(2)]
                        rrA = sb("rrA", [128, 8, 1], F32, pa)
                        tmpA = sb("tmpA", [128, 8, 64], F32, pa)
                        for q4 in range(4):
                            cx.dinc(sp.dma_start(
                                out=vat[:, q4 * 8:(q4 + 1) * 8, :],
                                in_=Va[s][q4 * 1024:(q4 + 1) * 1024, :].rearrange("(kt p) e -> p kt e", p=128)), s2["kv"])
                        kv_tk = [(s2["kv"], s2["kv"].n), kva_tk]
                        toff = (l * nseq + s) * NT
                        groups = [(t, kt) for t in range(NT) for kt in _na_kts(t)]
                        NG = len(groups)
                        goff = (l * nseq + s) * NG
                        ld_tk = {}
                        qk_tk = {}
                        ex_tk = {}
                        pv_tk = {}
                        accfree = {}
                        yst_tk = {}
                        lastqk_of_t = {}
                        chain_last = {}

                        def load_qa(t):
                            sl4 = (toff + t) % 4
                            cx.wait("sp", lastqk_of_t.get(t - 4))
                            a = cx.dinc(sp.dma_start(out=qa[sl4][:], in_=QTa[s][:, :, t * 128:(t + 1) * 128]), s2["ldqa"][sl4])
                            cx.wait("sp", chain_last.get(t - 4))
                            b_ = cx.dinc(sp.dma_start(out=ga[sl4][:], in_=Gs[s][t * 128:(t + 1) * 128, 0:512]), s2["ldga"][sl4])
                            ld_tk[t] = (a, b_)

                        def emit_qk_a(g):
                            t, kt = groups[g]
                            gg_ = goff + g
                            sbi = gg_ % 2
                            sl = (toff + t) % 4
                            cx.wait("pe", kv_tk)
                            cx.wait("pe", ld_tk[t][0])
                            cx.wait("pe", ex_tk.get(g - 2))
                            for j in range(4):
                                for par in range(2):
                                    o0 = (2 * sbi + par) * 512 + j * 128
                                    pe.matmul(ps[:, o0:o0 + 128], lhsT=kT[par * 64:(par + 1) * 64, j, kt * 128:(kt + 1) * 128],
                                              rhs=qa[sl][par * 64:(par + 1) * 64, j, :], start=(j == 0), stop=False,
                                              tile_position=(par * 64, 0), skip_group_check=True)
                            ci = _COMBO[(t, kt)]
                            pe.matmul(bank(2 * sbi), lhsT=ident[:], rhs=Cb[:, ci, 0:512], start=False, stop=True, skip_group_check=True)
                            ins = pe.matmul(bank(2 * sbi + 1), lhsT=ident[:], rhs=Cb[:, ci, 512:1024], start=False, stop=True,
                                            skip_group_check=True)
                            qk_tk[g] = cx.inc(ins, s2["aqk"])
                            if kt == _na_kts(t)[-1]:
                                lastqk_of_t[t] = qk_tk[g]

                        def emit_exp_a(g):
                            gg_ = goff + g
                            sbi = gg_ % 2
                            cx.wait("act", qk_tk[g])
                            ex_tk[g] = cx.inc(act.activation(out=PT[gg_ % 3][:], in_=ps[:, sbi * 1024:(sbi + 1) * 1024], func=AF.Exp,
                                                             bias=zero1[:], scale=1.0), s2["aex"])

                        def emit_pv_a(g):
                            t, kt = groups[g]
                            gg_ = goff + g
                            u = toff + t
                            aset = u % 2
                            kts = _na_kts(t)
                            cx.wait("pe", ex_tk[g])
                            if kt == kts[0]:
                                cx.wait("pe", accfree.get(t - 2))
                            ins = None
                            for hh in range(8):
                                o0 = (4 + 2 * aset + hh // 4) * 512 + (hh % 4) * 65
                                c0 = (hh % 2) * 512 + (hh // 2) * 128
                                ins = pe.matmul(ps[:, o0:o0 + 65], lhsT=PT[gg_ % 3][:, c0:c0 + 128], rhs=vat[:, kt, hh * 65:(hh + 1) * 65],
                                                start=(kt == kts[0] and hh % 4 == 0), stop=(kt == kts[-1]), skip_group_check=True)
                            pv_tk[g] = cx.inc(ins, s2["apv"])
                            if kt == kts[-1]:
                                emit_chain_a(t, aset, pv_tk[g])

                        def emit_chain_a(t, aset, acc_tk):
                            sl = (toff + t) % 2
                            sl4 = (toff + t) % 4
                            cx.wait("dve", acc_tk)
                            cx.wait("dve", cx.prev_chain_a)
                            accv = [bank(4 + 2 * aset + bk)[:, 0:260].rearrange("p (h e) -> p h e", h=4) for bk in range(2)]
                            cx.inc(dve.reciprocal(out=rrA[:, 0:4, :], in_=accv[0][:, :, 64:65]), s2["apost"])
                            k1 = cx.inc(dve.reciprocal(out=rrA[:, 4:8, :], in_=accv[1][:, :, 64:65]), s2["apost"])
                            cx.wait("dve", k1)
                            cx.inc(dve.tensor_tensor(out=tmpA[:, 0:4, :], in0=accv[0][:, :, 0:64],
                                                     in1=rrA[:, 0:4, :].broadcast_to([128, 4, 64]), op=ALU.mult), s2["apost"])
                            k2 = cx.inc(dve.tensor_tensor(out=tmpA[:, 4:8, :], in0=accv[1][:, :, 0:64],
                                                          in1=rrA[:, 4:8, :].broadcast_to([128, 4, 64]), op=ALU.mult), s2["apost"])
                            accfree[t] = k2
                            cx.wait("dve", k2)
                            cx.wait("dve", ld_tk[t][1])
                            cx.wait("dve", yst_tk.get(t - 2))
                            k3 = cx.inc(dve.tensor_tensor(out=yat[sl][:], in0=tmpA[:].rearrange("p h e -> p (h e)"), in1=ga[sl4][:],
                                                          op=ALU.mult), s2["apost"])
                            chain_last[t] = k3
                            cx.prev_chain_a = k3
                            cx.wait("sp", k3)
                            yst_tk[t] = cx.dinc(sp.dma_start(out=Ys[s][t * 128:(t + 1) * 128, 0:512], in_=yat[sl][:]), s2["ysta"][sl])
                            if t + 3 < NT:
                                load_qa(t + 3)

                        load_qa(0)
                        load_qa(1)
                        load_qa(2)
                        emit_qk_a(0)
                        emit_qk_a(1)
                        for g in range(NG):
                            emit_exp_a(g)
                            if g + 2 < NG:
                                emit_qk_a(g + 2)
                            emit_pv_a(g)
                        fin = [(sm, sm.n) for sm in s2["ysta"]]
                        for en in ("pe", "act", "dve", "pool", "sp"):
                            cx.wait(en, fin)
                        barrier()
                    if debug == "p2a":
                        continue

                    with ExitStack() as p3:
                        yt = [sb(f"yt{i}", [128, D], BF16, p3) for i in range(2)]
                        x3 = [sb(f"x3{i}", [128, D], F32, p3) for i in range(2)]
                        yT = [sb(f"yT{i}", [128, 8, 128], BF16, p3) for i in range(2)]
                        xo = [sb(f"xo{i}", [128, D], F32, p3) for i in range(2)]
                        of = [sb(f"of{i}", [128, D], F32, p3) for i in range(2)]
                        ss3 = sb("ss3", [128, NT], F32, p3)
                        tt3 = sb("tt3", [128, NT], F32, p3)
                        rs3 = sb("rs3", [128, NT], F32, p3)
                        toff = (l * nseq + s) * NT
                        ev_tk = {}
                        mm_tk = {}
                        add_tk = {}
                        st_tk = {}
                        tr_tk = {}
                        dst = out if (last_layer and final) else xs
                        def pre3(t):
                            u = toff + t
                            sl = u % 2
                            tok0 = s * S + t * 128
                            cx.wait("sp", tr_tk.get(t - 2))
                            ldy = cx.dinc(sp.dma_start(out=yt[sl][:], in_=Ys[s][t * 128:(t + 1) * 128, :]), s2["ldy"][sl])
                            cx.wait("sp", add_tk.get(t - 2))
                            ldx_tk[t] = cx.dinc(sp.dma_start(out=x3[sl][:], in_=xsrc[tok0:tok0 + 128, :]), s2["ldx3"][sl])
                            cx.wait("pe", ldy)
                            cx.wait("pe", ev_tk.get(t - 2))
                            trb = bank(sl).bitcast(BF16)
                            for c in range(8):
                                ins = pe.transpose(out=trb[:, c * 128:(c + 1) * 128], in_=yt[sl][:, c * 128:(c + 1) * 128],
                                                   identity=ident[:])
                            tr_tk[t] = cx.inc(ins, s2["tr3"])
                            cx.wait("act", tr_tk[t])
                            cx.wait("act", mm_tk.get(t - 2))
                            ev_tk[t] = cx.inc(act.activation(out=yT[sl][:].rearrange("p c t -> p (c t)"), in_=trb, func=AF.Copy),
                                              s2["ev3"])

                        def main3(t):
                            u = toff + t
                            sl = u % 2
                            tok0 = s * S + t * 128
                            cx.wait("pe", ev_tk[t])
                            cx.wait("pe", add_tk.get(t - 2))
                            for half in range(2):
                                for c in range(8):
                                    ins = pe.matmul(bank(2 + 2 * sl + half), lhsT=yT[sl][:, c, :], rhs=Wo[:, c, half * 512:(half + 1) * 512],
                                                    start=(c == 0), stop=(c == 7))
                            mm_tk[t] = cx.inc(ins, s2["mm3"])
                            cx.wait("dve", mm_tk[t])
                            cx.wait("dve", ldx_tk[t])
                            cx.wait("dve", st_tk.get(t - 2))
                            add_tk[t] = cx.inc(dve.tensor_tensor(out=xo[sl][:], in0=bank(2 + 2 * sl, 2), in1=x3[sl][:], op=ALU.add),
                                               s2["add3"])
                            if last_layer and final:
                                cx.wait("act", add_tk[t])
                                cx.wait("act", cx.junk_tk)
                                q1 = cx.inc(act.activation(out=junk[:], in_=xo[sl][:], func=AF.Square, accum_out=ss3[:, t:t + 1]),
                                            s2["sq3"])
                                cx.junk_tk = q1
                                cx.wait("pool", q1)
                                q2 = cx.inc(pool.tensor_scalar(out=tt3[:, t:t + 1], in0=ss3[:, t:t + 1], scalar1=1.0 / D, scalar2=1e-6,
                                                               op0=ALU.mult, op1=ALU.add), s2["rs3"])
                                cx.wait("pool", q2)
                                q3 = cx.inc(pool.tensor_tensor(out=rs3[:, t:t + 1], in0=tt3[:, t:t + 1], in1=mhalf[:, 0:1], op=ALU.pow),
                                            s2["rs3"])
                                cx.wait("dve", q3)
                                q4 = cx.inc(dve.scalar_tensor_tensor(out=of[sl][:], in0=xo[sl][:], scalar=rs3[:, t:t + 1], in1=fg_rep[:],
                                                                     op0=ALU.mult, op1=ALU.mult), s2["of3"])
                                cx.wait("pool", q4)
                                st_tk[t] = cx.dinc(pool.dma_start(out=dst[tok0:tok0 + 128, :], in_=of[sl][:]), s2["st3"][sl])
                            else:
                                cx.wait("pool", add_tk[t])
                                st_tk[t] = cx.dinc(pool.dma_start(out=dst[tok0:tok0 + 128, :], in_=xo[sl][:]), s2["st3"][sl])

                        ldx_tk = {}
                        pre3(0)
                        for t in range(NT):
                            if t + 1 < NT:
                                pre3(t + 1)
                            main3(t)
                        fin = [(sm, sm.n) for sm in s2["st3"]]
                        for en in ("pe", "act", "dve", "pool", "sp"):
                            cx.wait(en, fin)
                        barrier()
                    seqst.close()
            xsrc = xs

        barrier()
    return nc


def _prep_shared(inputs):
    f32 = np.float32
    t5 = np.asarray(inputs["t5_table"], f32)
    bidx = _t5_strip_idx()
    strip = np.ascontiguousarray(t5[bidx].transpose(0, 2, 1))
    far = np.concatenate([t5[15], t5[31]])[None, :].astype(f32)
    rpb = np.asarray(inputs["na_rpb"], f32)
    nl = rpb.shape[0]
    rpb_ext = np.concatenate([rpb.reshape(nl, -1), np.zeros((nl, 1), f32)], axis=1)
    na_R = np.ascontiguousarray(rpb_ext[:, _RIDX].transpose(0, 2, 1, 3))
    na_mask = np.ascontiguousarray(np.stack([m for (_, m) in _MASKS], axis=1))
    lamv = np.concatenate([np.asarray(inputs[k], f32) for k in ("lambda_q1", "lambda_k1", "lambda_q2", "lambda_k2")], axis=1)
    return {
        "w_in": np.ascontiguousarray(inputs["w_in"], f32),
        "w_out": np.ascontiguousarray(inputs["w_out"], f32),
        "norm_g": np.ascontiguousarray(inputs["norm_g"], f32),
        "final_g": np.ascontiguousarray(np.asarray(inputs["final_g"], f32)[None, :]),
        "subln_g": np.ascontiguousarray(inputs["subln_g"], f32),
        "lamv": np.ascontiguousarray(lamv),
        "t5_strip": strip.astype(f32),
        "t5_far": far,
        "na_R": na_R.astype(f32),
        "na_mask": na_mask.astype(f32),
        "ident": np.eye(128, dtype=f32),
    }


def kernel(**inputs):
    x = np.asarray(inputs["x"], np.float32)
    B = x.shape[0]
    per = B // NCORES
    shared = _prep_shared(inputs)
    nc = build_nc(nseq=per, nlayers=DEPTH)
    in_maps = []
    for c in range(NCORES):
        m = dict(shared)
        m["x"] = np.ascontiguousarray(x[c * per:(c + 1) * per].reshape(per * S, D))
        in_maps.append(m)
    res = run_bass_kernel_spmd(nc, in_maps, core_ids=list(range(NCORES)))
    outs = [np.asarray(r["out"], np.float32).reshape(per, S, D) for r in res.results]
    return np.concatenate(outs, axis=0)
```
